# Optimizing a Trainium2 kernel written in Bass

```python
import jax, jax.numpy as jnp
from jax import lax
import numpy as np

D_MODEL = 1024
BATCH = 8
SEQ = 4096
DEPTH = 2

PLE_DIM = 256
GRID_W = 64
N_MIXERS = 2
N_A_LAYERS = (DEPTH + 1) // 2
N_B_LAYERS = DEPTH // 2
SG_CHUNK = 128
SG_INNER = 2 * D_MODEL
SG_GROUPS = 8
NA_HEAD_DIM = 64
NA_HEADS = D_MODEL // NA_HEAD_DIM
NA_WIN_ROWS = 8
NA_WIN_COLS = 16
NA_COL_BLOCK = 16
NA_KEY_COLS = NA_COL_BLOCK + NA_WIN_COLS
FFN_HIDDEN = ((8 * D_MODEL // 3 + 127) // 128) * 128
CONV_WIDTH = 3
DEEPNORM_ALPHA = (2 * DEPTH) ** 0.25
DEEPNORM_BETA = (8 * DEPTH) ** -0.25
LN_EPS = 1e-5
NEG_INF = -1e30

kernel_name = "hybrid_gmlp_natten_convffn_deepnorm_encoder"


def layer_norm(x, g, b):
    xf = x.astype(jnp.float32)
    mu = jnp.mean(xf, axis=-1, keepdims=True)
    var = jnp.mean(jnp.square(xf - mu), axis=-1, keepdims=True)
    y = (xf - mu) * lax.rsqrt(var + LN_EPS) * g.astype(jnp.float32) + b.astype(jnp.float32)
    return y.astype(x.dtype)


def spatial_gating_mixer(x, w_in, b_in, ln_g, ln_b, w_s, b_s, w_out, b_out):
    b, s, _ = x.shape
    z = jax.nn.gelu(x @ w_in + b_in, approximate=False)
    u, v = jnp.split(z, 2, axis=-1)
    v = layer_norm(v, ln_g, ln_b)
    v = v.reshape(b, s // SG_CHUNK, SG_CHUNK, SG_GROUPS, SG_INNER // SG_GROUPS)
    v = jnp.einsum('gts,bnsgc->bntgc', w_s, v) + b_s.T[:, :, None]
    y = u * v.reshape(b, s, SG_INNER)
    return y @ w_out + b_out


def neighbourhood_attention(x, w_qkv, b_qkv, rpb, w_o, b_o):
    b, s, d = x.shape
    rows = s // GRID_W
    kh = min(NA_WIN_ROWS, rows)
    qkv = (x @ w_qkv + b_qkv).reshape(b, rows, GRID_W, 3, NA_HEADS, NA_HEAD_DIM)
    q = qkv[:, :, :, 0] * (NA_HEAD_DIM ** -0.5)
    k = qkv[:, :, :, 1]
    v = qkv[:, :, :, 2]
    qr = jnp.arange(rows)
    key_rows = jnp.clip(qr - kh // 2, 0, rows - kh)[:, None] + jnp.arange(kh)[None, :]
    dr_idx = key_rows - qr[:, None] + (NA_WIN_ROWS - 1)
    outs = []
    for q0 in range(0, GRID_W, NA_COL_BLOCK):
        k0 = min(max(q0 - NA_WIN_COLS // 2, 0), GRID_W - NA_KEY_COLS)
        qc = jnp.arange(q0, q0 + NA_COL_BLOCK)
        kc = jnp.arange(k0, k0 + NA_KEY_COLS)
        col_start = jnp.clip(qc - NA_WIN_COLS // 2, 0, GRID_W - NA_WIN_COLS)
        valid = (kc[None, :] >= col_start[:, None]) & (kc[None, :] < col_start[:, None] + NA_WIN_COLS)
        dc_idx = jnp.clip(kc[None, :] - qc[:, None], 1 - NA_WIN_COLS, NA_WIN_COLS - 1) + (NA_WIN_COLS - 1)
        bias = rpb[:, dr_idx[:, None, :, None], dc_idx[None, :, None, :]].astype(jnp.float32)
        bias = jnp.where(valid[None, None, :, None, :], bias, NEG_INF)
        qb = q[:, :, q0:q0 + NA_COL_BLOCK]
        kb = k[:, key_rows, k0:k0 + NA_KEY_COLS]
        vb = v[:, key_rows, k0:k0 + NA_KEY_COLS]
        logits = jnp.einsum('brqhd,brakhd->bhrqak', qb, kb).astype(jnp.float32) + bias
        probs = jax.nn.softmax(logits.reshape(logits.shape[:-2] + (-1,)), axis=-1)
        probs = probs.reshape(logits.shape).astype(vb.dtype)
        outs.append(jnp.einsum('bhrqak,brakhd->brqhd', probs, vb))
    o = jnp.concatenate(outs, axis=2).reshape(b, s, d)
    return o @ w_o + b_o


def conv_ffn(x, w_up, b_up, conv_w, conv_b, w_down, b_down):
    h = x @ w_up + b_up
    c = h.shape[-1]
    h = lax.conv_general_dilated(h, conv_w[:, None, :], window_strides=(1,),
                                 padding=[(CONV_WIDTH // 2, CONV_WIDTH // 2)],
                                 dimension_numbers=('NWC', 'WIO', 'NWC'),
                                 feature_group_count=c) + conv_b
    a, g = jnp.split(h, 2, axis=-1)
    return (jax.nn.gelu(a) * g) @ w_down + b_down


def setup_inputs(seed: int = 0) -> dict:
    key = jax.random.key(seed)
    ks = iter(jax.random.split(key, 40))

    def nrm(shape, scale):
        return scale * jax.random.normal(next(ks), shape, jnp.float32)

    D, E, F, G, H = D_MODEL, SG_INNER, FFN_HIDDEN, SG_GROUPS, NA_HEADS
    qkv_col_scale = jnp.concatenate([jnp.ones((2 * D,), jnp.float32),
                                     jnp.full((D,), DEEPNORM_BETA, jnp.float32)])
    return {
        "x": nrm((BATCH, SEQ, D), 1.0),
        "p": nrm((DEPTH, BATCH, SEQ, PLE_DIM), 1.0),
        "sg_w_in": nrm((N_A_LAYERS, D, 2 * E), D ** -0.5),
        "sg_b_in": nrm((N_A_LAYERS, 2 * E), 0.02),
        "sg_ln_g": 1.0 + nrm((N_A_LAYERS, E), 0.02),
        "sg_ln_b": nrm((N_A_LAYERS, E), 0.02),
        "sg_w_s": nrm((N_A_LAYERS, G, SG_CHUNK, SG_CHUNK), SG_CHUNK ** -0.5),
        "sg_b_s": 1.0 + nrm((N_A_LAYERS, G, SG_CHUNK), 0.02),
        "sg_w_out": nrm((N_A_LAYERS, E, D), DEEPNORM_BETA * E ** -0.5),
        "sg_b_out": nrm((N_A_LAYERS, D), 0.02),
        "na_w_qkv": nrm((N_B_LAYERS, D, 3 * D), D ** -0.5) * qkv_col_scale,
        "na_b_qkv": nrm((N_B_LAYERS, 3 * D), 0.02),
        "na_rpb": nrm((N_B_LAYERS, H, 2 * NA_WIN_ROWS - 1, 2 * NA_WIN_COLS - 1), 0.1),
        "na_w_o": nrm((N_B_LAYERS, D, D), DEEPNORM_BETA * D ** -0.5),
        "na_b_o": nrm((N_B_LAYERS, D), 0.02),
        "ln1_g": 1.0 + nrm((DEPTH, D), 0.02),
        "ln1_b": nrm((DEPTH, D), 0.02),
        "ln2_g": 1.0 + nrm((DEPTH, D), 0.02),
        "ln2_b": nrm((DEPTH, D), 0.02),
        "ffn_w_up": nrm((DEPTH, D, 2 * F), DEEPNORM_BETA * D ** -0.5),
        "ffn_b_up": nrm((DEPTH, 2 * F), 0.02),
        "ffn_conv_w": nrm((DEPTH, CONV_WIDTH, 2 * F), CONV_WIDTH ** -0.5),
        "ffn_conv_b": nrm((DEPTH, 2 * F), 0.02),
        "ffn_w_down": nrm((DEPTH, F, D), DEEPNORM_BETA * F ** -0.5),
        "ffn_b_down": nrm((DEPTH, D), 0.02),
        "ple_w_proj": nrm((DEPTH, PLE_DIM, D), PLE_DIM ** -0.5),
        "ple_w_gate": nrm((DEPTH, D, D), D ** -0.5),
        "ple_b_gate": nrm((DEPTH, D), 0.02),
    }


def reference(x, p, sg_w_in, sg_b_in, sg_ln_g, sg_ln_b, sg_w_s, sg_b_s, sg_w_out, sg_b_out,
              na_w_qkv, na_b_qkv, na_rpb, na_w_o, na_b_o,
              ln1_g, ln1_b, ln2_g, ln2_b,
              ffn_w_up, ffn_b_up, ffn_conv_w, ffn_conv_b, ffn_w_down, ffn_b_down,
              ple_w_proj, ple_w_gate, ple_b_gate):
    for i in range(DEPTH):
        j = i // N_MIXERS
        if i % N_MIXERS == 0:
            mix = spatial_gating_mixer(x, sg_w_in[j], sg_b_in[j], sg_ln_g[j], sg_ln_b[j],
                                       sg_w_s[j], sg_b_s[j], sg_w_out[j], sg_b_out[j])
        else:
            mix = neighbourhood_attention(x, na_w_qkv[j], na_b_qkv[j], na_rpb[j],
                                          na_w_o[j], na_b_o[j])
        x = layer_norm(DEEPNORM_ALPHA * x + mix, ln1_g[i], ln1_b[i])
        ffn = conv_ffn(x, ffn_w_up[i], ffn_b_up[i], ffn_conv_w[i], ffn_conv_b[i],
                       ffn_w_down[i], ffn_b_down[i])
        x = layer_norm(DEEPNORM_ALPHA * x + ffn, ln2_g[i], ln2_b[i])
        gate = jax.nn.sigmoid(x @ ple_w_gate[i] + ple_b_gate[i])
        x = x + gate * (p[i] @ ple_w_proj[i])
    return x
```

```python
import os
import numpy as np
import concourse.bass as bass
import concourse.mybir as mybir
from concourse.bass_utils import run_bass_kernel_spmd
from contextlib import ExitStack

F32 = mybir.dt.float32
BF16 = mybir.dt.bfloat16
AF = mybir.ActivationFunctionType
ALU = mybir.AluOpType

D = 1024
SEQ = 4096
NB = 8
FF = 2816
NJ = 22
ALPHA = float(4.0 ** 0.25)
EPS = 1e-5
NEG = -30000.0
EPOCH = 4096

COLS = {}


def _mkcols():
    off = 0

    def add(name, n):
        nonlocal off
        COLS[name] = off
        off += n

    add("bu_in", 16)
    add("lng", 16)
    for l in (0, 1):
        for nm in ("ln1g", "ln1b", "ln2g", "ln2b", "bgate"):
            add(f"{nm}{l}", 8)
        for nm in ("bua", "bug", "w0a", "w1a", "w2a", "w0g", "w1g", "w2g", "cba", "cbg"):
            add(f"{nm}{l}", NJ)
    add("bq", 8)
    add("bk", 8)
    return off


NCOL = _mkcols()
DC = {}


def _mkd():
    off = 0

    def add(name, n):
        nonlocal off
        DC[name] = off
        off += n

    add("eps", 1)
    add("ones", 128)
    for l in (0, 1):
        for nm in ("cKa", "cKg", "ne0a", "ne0g", "ne2a", "ne2g", "tmp"):
            add(f"{nm}{l}", NJ)
        add(f"hbg{l}", 8)
    return off


NDC = _mkd()
R_BINV, R_BV, R_BOUT, R_BDN0, R_BDN1, R_BO, R_BS, R_LNB = 0, 2048, 3072, 4096, 5120, 6144, 7168, 8192
NROW = 10240


class T:
    __slots__ = ("name", "w", "r")

    def __init__(self, name=""):
        self.name = name
        self.w = None
        self.r = []


class Sched:
    ENGS = ("pe", "act", "dve", "pool", "sp")

    def __init__(self, nc, es, plan=None):
        import bisect
        self._bisect = bisect
        self.nc = nc
        self.es = es
        self.plan = plan
        self.cnt = {e: 0 for e in self.ENGS}
        self.inc = {e: 0 for e in self.ENGS}
        self.sems = {e: [] for e in self.ENGS}
        self.waited = {e: {} for e in self.ENGS}
        self.last = {e: None for e in self.ENGS}
        self.needed = {e: set() for e in self.ENGS}
        self.dma_toks = {}
        self.nsem = 0
        self.eng = {"pe": nc.tensor, "act": nc.scalar, "dve": nc.vector, "pool": nc.gpsimd, "sp": nc.sync}

    def new_sem(self, name):
        self.nsem += 1
        return self.es.enter_context(self.nc.semaphore(f"{name}_{self.nsem}"))

    def dsem(self, name):
        return [self.new_sem(name), 0]

    def _eng_sem(self, e, k):
        while len(self.sems[e]) <= k:
            self.sems[e].append(self.new_sem(f"s_{e}_{len(self.sems[e])}"))
        return self.sems[e][k]

    @staticmethod
    def _deps(reads, writes):
        deps = []
        for t in reads:
            if t.w is not None:
                deps.append(t.w)
        for t in writes:
            if t.w is not None:
                deps.append(t.w)
            deps.extend(t.r)
        return deps

    def _value(self, src, idx):
        if self.plan is None:
            return idx + 1
        return self._bisect.bisect_right(self.plan[src], idx)

    def _emit_waits(self, e, deps):
        best = {}
        for tok in deps:
            if tok[0] == "dma":
                key = ("dma", id(tok[1]))
                if key not in best or best[key][2] < tok[2]:
                    best[key] = tok
            else:
                src, idx = tok
                if src == "pe" and e == "pe":
                    continue
                if src not in best or best[src][1] < idx:
                    best[src] = tok
        wd = self.waited[e]
        for key, tok in best.items():
            v = tok[2] if tok[0] == "dma" else tok[1]
            if wd.get(key, -1) >= v:
                continue
            wd[key] = v
            if tok[0] == "dma":
                self.eng[e].wait_ge(tok[1], tok[2])
            else:
                src, idx = tok
                self.needed[src].add(idx)
                val = self._value(src, idx)
                self.eng[e].wait_ge(self._eng_sem(src, (val - 1) // EPOCH), (val - 1) % EPOCH + 1)

    @staticmethod
    def _mark(tok, reads, writes):
        for t in reads:
            t.r.append(tok)
        for t in writes:
            t.w = tok
            t.r = []

    def op(self, e, fn, reads=(), writes=()):
        self._emit_waits(e, self._deps(reads, writes))
        i = self.cnt[e]
        self.cnt[e] += 1
        ins = fn(self.eng[e])
        if self.plan is None:
            ins.then_inc(self._eng_sem(e, i // EPOCH), 1)
        else:
            pl = self.plan[e]
            k = self._bisect.bisect_left(pl, i)
            if k < len(pl) and pl[k] == i:
                self.inc[e] += 1
                v = self.inc[e]
                assert v == k + 1
                ins.then_inc(self._eng_sem(e, (v - 1) // EPOCH), 1)
        tok = (e, i)
        self.last[e] = tok
        self._mark(tok, reads, writes)
        return tok

    def dma(self, e, fns, ds, reads=(), writes=(), track=True):
        self._emit_waits(e, self._deps(reads, writes))
        for fn in fns:
            ds[1] += 16
            fn(self.eng[e]).then_inc(ds[0], 16)
        tok = ("dma", ds[0], ds[1])
        if track:
            self.dma_toks[id(ds[0])] = tok
        self._mark(tok, reads, writes)
        return tok

    def barrier(self):
        toks = [t for t in self.last.values() if t is not None] + list(self.dma_toks.values())
        for e in self.ENGS:
            self._emit_waits(e, toks)

    def get_plan(self):
        return {e: sorted(self.needed[e]) for e in self.ENGS}


class Ring:
    uid = 0

    def __init__(self, S, nc, es, name, nslots, shape, srcs, src_T):
        self.S = S
        self.n = nslots
        Ring.uid += 1
        name = f"{name}_{Ring.uid}"
        self.tile = es.enter_context(nc.sbuf_tensor(name, [128, nslots] + list(shape), BF16))
        self.Ts = [T(f"{name}{k}") for k in range(nslots)]
        self.ds = [S.dsem(f"d_{name}{k}") for k in range(nslots)]
        self.srcs = srcs
        self.src_T = src_T
        self.issued = 0

    def ahead(self, k):
        if k < len(self.srcs):
            self.get(k)

    def get(self, k):
        while self.issued < len(self.srcs) and self.issued <= k + self.n - 1:
            j = self.issued
            slot = j % self.n
            src = self.srcs[j]
            dst = self.tile[:, slot]
            if isinstance(src, tuple):
                src, sub = src
                dst = sub(dst)
            self.S.dma("sp", [lambda e, dst=dst, src=src: e.dma_start(out=dst, in_=src, allow_slow_non_contiguous=True)], self.ds[slot],
                       reads=self.src_T, writes=[self.Ts[slot]])
            self.issued += 1
        return k % self.n


def build_program(stop=None):
    _, plan = _build_once(stop, None)
    nc, _ = _build_once(stop, plan)
    return nc


def _build_once(stop, plan):
    nc = bass.Bass("TRN2", target_bir_lowering=False)

    def dram(name, shape, dtype=F32, kind="ExternalInput"):
        return nc.dram_tensor(name, list(shape), dtype, kind=kind).ap()

    xT = dram("xT", [8, 128, SEQ])
    pT = dram("pT", [2, 2, 128, SEQ])
    wu = dram("wu", [16, 128, 8, 128])
    wv = dram("wv", [128, 8, 2048])
    wso = dram("wso", [8, 128, 16, 128])
    wup = dram("wup", [2, NJ, 128, 8, 256])
    wdn = dram("wdn", [2, 8, 128, NJ, 128])
    wgp = dram("wgp", [2, 8, 128, 10, 128])
    wqk = dram("wqk", [16, 128, 8, 128])
    wvn = dram("wvn", [2, 128, 8, 512])
    won = dram("won", [8, 128, 8, 128])
    wst = dram("wst", [128, 8, 128])
    cols = dram("cols", [128, NCOL])
    rows = dram("rows", [1, NROW])
    rpbt = dram("rpbt", [16, 128, 21, 128])
    ident = dram("ident", [128, 128])
    outT = dram("outT", [8, 128, SEQ], kind="ExternalOutput")
    wu_b = dram("wu_b", [16, 128, 8, 128], BF16, "Internal")
    wso_b = dram("wso_b", [8, 128, 16, 128], BF16, "Internal")
    wup_b = dram("wup_b", [2, NJ, 128, 8, 256], BF16, "Internal")
    wdn_b = dram("wdn_b", [2, 8, 128, NJ, 128], BF16, "Internal")
    wgp_b = dram("wgp_b", [2, 8, 128, 10, 128], BF16, "Internal")
    wqk_b = dram("wqk_b", [16, 128, 8, 128], BF16, "Internal")
    wvn_b = dram("wvn_b", [2, 128, 8, 512], BF16, "Internal")
    won_b = dram("won_b", [8, 128, 8, 128], BF16, "Internal")
    rpbt_b = dram("rpbt_b", [16, 128, 21, 128], BF16, "Internal")
    pT_b = dram("pT_b", [2, 2, 128, SEQ], BF16, "Internal")

    with ExitStack() as es:
        S = Sched(nc, es, plan)

        _uid = [0]

        def sb(name, shape, dtype, st=es):
            _uid[0] += 1
            return st.enter_context(nc.sbuf_tensor(f"{name}_{_uid[0]}", list(shape), dtype))

        xs = sb("xs", [128, 8, SEQ], BF16)
        colsb = sb("colsb", [128, NCOL], F32)
        dcols = sb("dcols", [128, NDC], F32)
        identb = sb("identb", [128, 128], BF16)
        onesb = sb("onesb", [128, 512], BF16)
        mmb = sb("mmb", [128, 128], BF16)
        P2 = [es.enter_context(nc.psum_tensor(f"P2_{k}", [128, 1024], F32)) for k in range(4)]

        def bank(k):
            return P2[k // 2][:, (k % 2) * 512:(k % 2) * 512 + 512]

        tbank = [T(f"bank{k}") for k in range(8)]
        txs = [T(f"xs{k}") for k in range(16)]

        def tx(lo, hi):
            return txs[lo // 256:(hi - 1) // 256 + 1]

        tcols, tdcols, tident, tones, tmm = T("cols"), T("dcols"), T("ident"), T("ones"), T("mm")

        def col(name, j=0):
            o = COLS[name] + j
            return colsb[:, o:o + 1]

        def dcol(name, j=0):
            o = DC[name] + j
            return dcols[:, o:o + 1]

        d_misc = S.dsem("d_misc")
        S.dma("sp", [lambda e: e.dma_start(out=colsb[:], in_=cols)], d_misc, writes=[tcols])
        d_x = S.dsem("d_x")
        S.dma("pool", [lambda e, kc=kc: e.dma_start(out=xs[:, kc, :], in_=xT[kc]) for kc in range(8)], d_x, writes=txs)
        d_id = S.dsem("d_id")
        S.dma("pool", [lambda e: e.dma_start(out=identb[:], in_=ident)], d_id, writes=[tident])
        S.op("dve", lambda e: e.memset(onesb[:], 1.0), writes=[tones])
        S.op("dve", lambda e: e.memset(mmb[:], 1.0 / 1024.0), writes=[tmm])
        S.op("dve", lambda e: e.memset(dcols[:, DC["eps"]:DC["eps"] + 1], EPS), writes=[tdcols])
        S.op("dve", lambda e: e.memset(dcols[:, DC["ones"]:DC["ones"] + 128], 1.0), writes=[tdcols])

        def cast_group(name, pairs):
            ds = S.dsem("dc_" + name)
            t = T("scr_" + name)
            S.dma("pool", [lambda e, o=o, i=i: e.dma_start(out=o, in_=i) for (o, i) in pairs], ds, writes=[t], track=False)
            return t

        t_wvb_src = None
        casts = {}

        def do_casts(which):
            if which == "A":
                casts["wu"] = cast_group("wu", [(wu_b[k], wu[k]) for k in range(16)])
                casts["wso"] = cast_group("wso", [(wso_b[k], wso[k]) for k in range(8)])
            elif which in ("F0", "F1"):
                l = int(which[1])
                casts[f"wup{l}"] = cast_group(f"wup{l}", [(wup_b[l, j], wup[l, j]) for j in range(NJ)])
                casts[f"wdn{l}"] = cast_group(f"wdn{l}", [(wdn_b[l, k], wdn[l, k]) for k in range(8)])
                casts[f"wgp{l}"] = cast_group(f"wgp{l}", [(wgp_b[l, k], wgp[l, k]) for k in range(8)])
                casts[f"pT{l}"] = cast_group(f"pT{l}", [(pT_b[l, kk], pT[l, kk]) for kk in range(2)])
            elif which == "B":
                casts["wqk"] = cast_group("wqk", [(wqk_b[k], wqk[k]) for k in range(16)])
                casts["wvn"] = cast_group("wvn", [(wvn_b[k], wvn[k]) for k in range(2)])
                casts["won"] = cast_group("won", [(won_b[k], won[k]) for k in range(8)])

        def layer_norm_gen(pre, n, tpre, prebf, sq, tprebf, tsq, lt, affine, banks=(6, 7), norm_eng=("dve", "dve")):
            mean_sb, var, rstd, nmr, tl = lt
            S.op("act", lambda e: e.activation(out=prebf[:, :, 0:n], in_=pre[:, :, 0:n], func=AF.Copy),
                 reads=tpre, writes=tprebf)
            yield
            S.op("act", lambda e: e.activation(out=sq[:, :, 0:n], in_=pre[:, :, 0:n], func=AF.Square),
                 reads=tpre, writes=tsq)
            yield
            bm, bq = bank(banks[0]), bank(banks[1])
            for dc in range(8):
                S.op("pe", lambda e, dc=dc: e.matmul(bm[:, 0:n], lhsT=mmb[:], rhs=prebf[:, dc, 0:n],
                                                     start=(dc == 0), stop=(dc == 7)),
                     reads=tprebf + [tmm], writes=[tbank[banks[0]]])
            yield
            for dc in range(8):
                S.op("pe", lambda e, dc=dc: e.matmul(bq[:, 0:n], lhsT=mmb[:], rhs=sq[:, dc, 0:n],
                                                     start=(dc == 0), stop=(dc == 7)),
                     reads=tsq + [tmm], writes=[tbank[banks[1]]])
            yield
            S.op("act", lambda e: e.activation(out=mean_sb[:, 0:n], in_=bm[:, 0:n], func=AF.Copy),
                 reads=[tbank[banks[0]]], writes=[tl[0]])
            S.op("dve", lambda e: e.tensor_tensor(out=var[:, 0:n], in0=mean_sb[:, 0:n], in1=mean_sb[:, 0:n], op=ALU.mult),
                 reads=[tl[0]], writes=[tl[1]])
            S.op("dve", lambda e: e.tensor_tensor(out=var[:, 0:n], in0=bq[:, 0:n], in1=var[:, 0:n], op=ALU.subtract),
                 reads=[tbank[banks[1]], tl[1]], writes=[tl[1]])
            yield
            S.op("act", lambda e: e.activation(out=rstd[:, 0:n], in_=var[:, 0:n], func=AF.Sqrt, bias=dcol("eps"), scale=1.0),
                 reads=[tl[1], tdcols], writes=[tl[2]])
            S.op("dve", lambda e: e.reciprocal(out=rstd[:, 0:n], in_=rstd[:, 0:n]), reads=[tl[2]], writes=[tl[2]])
            S.op("dve", lambda e: e.scalar_tensor_tensor(out=nmr[:, 0:n], in0=mean_sb[:, 0:n], scalar=-1.0, in1=rstd[:, 0:n],
                                                         op0=ALU.mult, op1=ALU.mult),
                 reads=[tl[0], tl[2]], writes=[tl[3]])
            yield
            for dc in range(8):
                S.op(norm_eng[0], lambda e, dc=dc: e.tensor_tensor(out=pre[:, dc, 0:n], in0=pre[:, dc, 0:n], in1=rstd[:, 0:n], op=ALU.mult),
                     reads=[tpre[dc], tl[2]], writes=[tpre[dc]])
                S.op(norm_eng[1], lambda e, dc=dc: e.tensor_tensor(out=pre[:, dc, 0:n], in0=pre[:, dc, 0:n], in1=nmr[:, 0:n], op=ALU.add),
                     reads=[tpre[dc], tl[3]], writes=[tpre[dc]])
                affine(dc)
                yield

        def layer_norm(*a, **k):
            for _ in layer_norm_gen(*a, **k):
                pass

        def alloc_ln_temps(st, w):
            tiles = [sb(f"ln_{k}", [128, w], F32, st) for k in range(4)]
            return tiles + [[T(f"ln{k}") for k in range(4)]]

        def phase_A():
            do_casts("A")
            with ExitStack() as st:
                wvb = sb("wvb", [128, 8, 2048], BF16, st)
                wsb = sb("wsb", [128, 8, 128], BF16, st)
                Rt = sb("Rt", [128, 16, 128], F32, st)
                rowA = sb("rowA", [1, 3072], BF16, st)
                u = sb("u", [128, 16, 512], BF16, st)
                v32 = [sb(f"v32_{k}", [128, 2048], F32, st) for k in range(2)]
                vn = [sb(f"vn{k}", [128, 2048], BF16, st) for k in range(2)]
                tmpy = [sb(f"tmpy{k}", [128, 512], F32, st) for k in range(2)]
                pre = sb("preA", [128, 8, 512], F32, st)
                stt = [sb(f"stt{k}", [128, 32], F32, st) for k in range(2)]
                lt = alloc_ln_temps(st, 512)
                twvb, twsb, tRt, trowA = T(), T(), T(), T()
                tpre = [T() for _ in range(8)]
                tvn, tstt = [T(), T()], [T(), T()]
                tu = [T(f"u{k}") for k in range(16)]
                tv32 = [T(), T()]
                ttmpy = [T(), T()]
                ring_u = Ring(S, nc, st, "ru", 4, [8, 128], [wu_b[fc] for _ in range(NB) for fc in range(16)], [casts["wu"]])
                ring_wo = Ring(S, nc, st, "rwo", 2, [16, 128], [wso_b[dc] for _ in range(NB) for dc in range(8)], [casts["wso"]])
                dA = S.dsem("dA")
                S.dma("pool", [lambda e, kc=kc: e.dma_start(out=wvb[:, kc, :], in_=wv[:, kc, :]) for kc in range(8)], dA, writes=[twvb])
                dA2 = S.dsem("dA2")
                S.dma("pool", [lambda e: e.dma_start(out=wsb[:], in_=wst)], dA2, writes=[twsb])
                dA3 = S.dsem("dA3")
                S.dma("pool", [lambda e: e.dma_start(out=rowA[0:1, 0:2048], in_=rows[0:1, R_BINV:R_BINV + 2048]),
                               lambda e: e.dma_start(out=rowA[0:1, 2048:3072], in_=rows[0:1, R_BOUT:R_BOUT + 1024])],
                      dA3, writes=[trowA])
                for w_ in ("F0", "B", "F1"):
                    do_casts(w_)
                dA4 = S.dsem("dA4")
                S.dma("sp", [lambda e, a=a: e.dma_start(out=pre[:, a, :],
                                                        in_=rows[0:1, R_LNB + a * 512:R_LNB + a * 512 + 512].to_broadcast([128, 512]))
                             for a in range(4)]
                      + [lambda e, a=a: e.dma_start(out=pre[0:1, 4 + a, :], in_=rows[0:1, R_BS + a * 512:R_BS + a * 512 + 512])
                         for a in range(2)]
                      + [lambda e, a=a: e.dma_start(out=pre[:, 6 + a, :].rearrange("p (g t) -> p g t", g=4), in_=wst[:, 4 * a:4 * a + 4, :])
                         for a in range(2)],
                      dA4, writes=tpre)
                for cc in range(16):
                    g = cc // 2
                    pb = bank(cc % 2)
                    S.op("pe", lambda e, cc=cc, g=g, pb=pb: e.matmul(
                        pb[:, 0:128], lhsT=pre[:, cc // 4, (cc % 4) * 128:(cc % 4) * 128 + 128],
                        rhs=pre[:, 6 + g // 4, (g % 4) * 128:(g % 4) * 128 + 128], start=True, stop=False),
                        reads=tpre, writes=[tbank[cc % 2]])
                    S.op("pe", lambda e, cc=cc, g=g, pb=pb: e.matmul(
                        pb[:, 0:128], lhsT=dcols[0:1, DC["ones"]:DC["ones"] + 128],
                        rhs=pre[0:1, 4 + g // 4, (g % 4) * 128:(g % 4) * 128 + 128], start=False, stop=True),
                        reads=tpre + [tdcols], writes=[tbank[cc % 2]])
                    S.op("act", lambda e, cc=cc, pb=pb: e.activation(out=Rt[:, cc, :], in_=pb[:, 0:128], func=AF.Copy),
                         reads=[tbank[cc % 2]], writes=[tRt])

                prebf_v = v32[0][:].bitcast(BF16).rearrange("p (a b) -> p a b", a=8)
                sq_v = v32[1][:].bitcast(BF16).rearrange("p (a b) -> p a b", a=8)
                pending = None
                for i in range(NB):
                    t0 = 512 * i
                    txi = tx(t0, t0 + 512)
                    for fc in range(16):
                        slot = ring_u.get(i * 16 + fc)
                        pb = bank(fc % 2)
                        for kc in range(8):
                            S.op("pe", lambda e, pb=pb, slot=slot, kc=kc: e.matmul(
                                pb, lhsT=ring_u.tile[:, slot, kc, :], rhs=xs[:, kc, t0:t0 + 512], start=(kc == 0), stop=(kc == 7)),
                                reads=[ring_u.Ts[slot]] + txi, writes=[tbank[fc % 2]])
                        S.op("act", lambda e, pb=pb, fc=fc: e.activation(out=u[:, fc, :], in_=pb, func=AF.Gelu,
                                                                          bias=col("bu_in", fc), scale=1.0),
                             reads=[tbank[fc % 2], tcols], writes=[tu[fc]])
                        if pending is not None:
                            for _ in range(2):
                                next(pending, None)
                    if pending is not None:
                        for _ in pending:
                            pass
                        pending = None

                    def vmm(c, fbs):
                        tc0 = t0 + 128 * c
                        vb, tvb = v32[c % 2], tv32[c % 2]
                        sttc = stt[c % 2]
                        for fb in fbs:
                            pb = bank(2 + fb % 2)
                            for kc in range(8):
                                S.op("pe", lambda e, pb=pb, kc=kc, fb=fb: e.matmul(
                                    pb, lhsT=xs[:, kc, tc0:tc0 + 128], rhs=wvb[:, kc, fb * 512:fb * 512 + 512],
                                    start=(kc == 0), stop=False),
                                    reads=[twvb] + txi, writes=[tbank[2 + fb % 2]])
                            S.op("pe", lambda e, pb=pb, fb=fb: e.matmul(
                                pb, lhsT=onesb[0:1, 0:128], rhs=rowA[0:1, fb * 512:fb * 512 + 512], start=False, stop=True),
                                reads=[tones, trowA], writes=[tbank[2 + fb % 2]])
                            S.op("act", lambda e, pb=pb, fb=fb, vb=vb: e.activation(out=vb[:, fb * 512:fb * 512 + 512], in_=pb, func=AF.Gelu),
                                 reads=[tbank[2 + fb % 2]], writes=[tvb])
                            S.op("dve", lambda e, fb=fb, vb=vb, sttc=sttc: e.bn_stats(out=sttc[:, fb * 6:fb * 6 + 6], in_=vb[:, fb * 512:fb * 512 + 512]),
                                 reads=[tvb], writes=[tstt[c % 2]])

                    def stats_norm(c):
                        vb, tvb = v32[c % 2], tv32[c % 2]
                        sttc, ts = stt[c % 2], tstt[c % 2]
                        S.op("dve", lambda e: e.bn_aggr(out=sttc[:, 24:26], in_=sttc[:, 0:24]), reads=[ts], writes=[ts])
                        S.op("act", lambda e: e.activation(out=sttc[:, 26:27], in_=sttc[:, 25:26], func=AF.Sqrt, bias=dcol("eps"), scale=1.0),
                             reads=[ts, tdcols], writes=[ts])
                        S.op("dve", lambda e: e.reciprocal(out=sttc[:, 26:27], in_=sttc[:, 26:27]), reads=[ts], writes=[ts])
                        S.op("dve", lambda e: e.scalar_tensor_tensor(out=sttc[:, 27:28], in0=sttc[:, 24:25], scalar=-1.0, in1=sttc[:, 26:27],
                                                                     op0=ALU.mult, op1=ALU.mult), reads=[ts], writes=[ts])
                        S.op("act", lambda e: e.activation(out=vn[c % 2][:], in_=vb[:], func=AF.Identity, bias=sttc[:, 27:28], scale=sttc[:, 26:27]),
                             reads=[tvb, ts], writes=[tvn[c % 2]])

                    def spatial(c):
                        vnc, tvnc = vn[c % 2], tvn[c % 2]
                        for q4 in range(4):
                            pb = bank(4 + q4)
                            for k in range(4):
                                cc = q4 * 4 + k
                                g = cc // 2
                                S.op("pe", lambda e, pb=pb, k=k, cc=cc, g=g: e.matmul(
                                    pb[:, k * 128:k * 128 + 128], lhsT=vnc[:, cc * 128:cc * 128 + 128], rhs=wsb[:, g, :], start=True, stop=True),
                                    reads=[tvnc, twsb], writes=[tbank[4 + q4]])
                        for q4 in range(4):
                            pb = bank(4 + q4)
                            ty, tty = tmpy[q4 % 2], ttmpy[q4 % 2]
                            for k in range(4):
                                cc = q4 * 4 + k
                                S.op("act", lambda e, pb=pb, k=k, cc=cc, ty=ty: e.activation(
                                    out=ty[:, k * 128:k * 128 + 128], in_=pb[:, k * 128:k * 128 + 128], func=AF.Identity, scale=col("lng", cc)),
                                    reads=[tbank[4 + q4], tcols], writes=[tty])
                            S.op("dve", lambda e, ty=ty, q4=q4: e.tensor_tensor(
                                out=ty[:], in0=ty[:], in1=Rt[:, q4 * 4:q4 * 4 + 4, :].rearrange("p a b -> p (a b)"), op=ALU.add),
                                reads=[tRt], writes=[tty])
                            S.op("dve", lambda e, ty=ty, q4=q4: e.tensor_tensor(
                                out=u[:, q4 * 4:q4 * 4 + 4, c * 128:c * 128 + 128], in0=ty[:].rearrange("p (a b) -> p a b", a=4),
                                in1=u[:, q4 * 4:q4 * 4 + 4, c * 128:c * 128 + 128], op=ALU.mult),
                                reads=[tty] + tu[q4 * 4:q4 * 4 + 4], writes=tu[q4 * 4:q4 * 4 + 4])

                    vmm(0, range(4))
                    for c in range(4):
                        if c < 3:
                            vmm(c + 1, (0, 1))
                        stats_norm(c)
                        if c < 3:
                            vmm(c + 1, (2, 3))
                        spatial(c)
                    for dc in range(8):
                        slot = ring_wo.get(i * 8 + dc)
                        pb = bank(6 + dc % 2)
                        for fc in range(16):
                            S.op("pe", lambda e, pb=pb, slot=slot, fc=fc: e.matmul(
                                pb, lhsT=ring_wo.tile[:, slot, fc, :], rhs=u[:, fc, :], start=(fc == 0), stop=False),
                                reads=[ring_wo.Ts[slot], tu[fc]], writes=[tbank[6 + dc % 2]])
                        S.op("pe", lambda e, pb=pb, dc=dc: e.matmul(
                            pb, lhsT=rowA[0:1, 2048 + dc * 128:2048 + dc * 128 + 128], rhs=onesb[0:1, 0:512], start=False, stop=True),
                            reads=[trowA, tones], writes=[tbank[6 + dc % 2]])
                        S.op("dve", lambda e, pb=pb, dc=dc: e.scalar_tensor_tensor(
                            out=pre[:, dc, :], in0=xs[:, dc, t0:t0 + 512], scalar=ALPHA, in1=pb, op0=ALU.mult, op1=ALU.add),
                            reads=[tbank[6 + dc % 2]] + txi, writes=[tpre[dc]])

                    def affine(dc, t0=t0, txi=txi):
                        S.op("act", lambda e, dc=dc: e.activation(out=xs[:, dc, t0:t0 + 512], in_=pre[:, dc, :], func=AF.Identity,
                                                                  bias=col("ln1b0", dc), scale=col("ln1g0", dc)),
                             reads=[tpre[dc], tcols], writes=txi)

                    pending = layer_norm_gen(pre, 512, tpre, prebf_v, sq_v, [tv32[0]], [tv32[1]], lt, affine)
                for _ in pending:
                    pass
                S.barrier()

        t_rpscr = T("rpbt_scr")

        def bias_exp_gen(st):
            bt32 = sb("bt32", [128, 11, 128], F32, st)
            bte = sb("bte", [128, 11, 128], BF16, st)
            tbt32, tbte = T(), T()
            dbt = S.dsem("dbt")
            dscr = S.dsem("dscr")
            for h in range(16):
                for (lo, n_) in ((0, 11), (11, 10)):
                    S.dma("sp", [lambda e, h=h, lo=lo, n_=n_: e.dma_start(out=bt32[:, 0:n_, :], in_=rpbt[h, :, lo:lo + n_, :])], dbt, writes=[tbt32])
                    yield
                    S.op("act", lambda e, n_=n_: e.activation(out=bte[:, 0:n_, :], in_=bt32[:, 0:n_, :], func=AF.Exp), reads=[tbt32], writes=[tbte])
                    S.dma("sp", [lambda e, h=h, lo=lo, n_=n_: e.dma_start(out=rpbt_b[h, :, lo:lo + n_, :], in_=bte[:, 0:n_, :])], dscr,
                          reads=[tbte], writes=[t_rpscr])
                    yield

        def phase_F(l, final):
            with ExitStack() as st:
                NCB = 4
                act = sb("act", [128, NJ, 514], BF16, st)
                cbig = sb("cbig", [128, NCB, 2, 514], F32, st)
                cag = [cbig[:, k] for k in range(NCB)]
                ga = [sb(f"ga{k}", [128, 514], F32, st) for k in range(3)]
                carry = [sb(f"carry{k}", [128, 2, NJ], F32, st) for k in range(2)]
                hl = [sb(f"hl{k}", [128, 2, NJ], F32, st) for k in range(2)]
                rowF = sb("rowF", [1, 1024], BF16, st)
                pres = [sb(f"preF{k}", [128, 8, 512], F32, st) for k in range(2)]
                prebf = cbig[:, 0:2].rearrange("p a b c -> p (a b c)").bitcast(BF16)[:, 0:4096].rearrange("p (a b) -> p a b", a=8)
                sq = cbig[:, 2:4].rearrange("p a b c -> p (a b c)").bitcast(BF16)[:, 0:4096].rearrange("p (a b) -> p a b", a=8)
                sg = [sb(f"sg{k}", [128, 512], F32, st) for k in range(2)]
                lt = alloc_ln_temps(st, 512)
                if os.environ.get("KVERBOSE"):
                    print("phase F sbuf remaining before rings", nc.sbuf_bytes_remaining)
                tact = [T(f"act{j}") for j in range(NJ)]
                tcag, tga = [T() for _ in range(NCB)], [T() for _ in range(3)]
                tcarry, thl = [T(), T()], [T(), T()]
                trowF = T()
                tpres = [[T() for _ in range(8)] for _ in range(2)]
                tsg = [T(), T()]
                segs_all = []
                for i in range(NB):
                    t0 = 512 * i
                    if i == 0:
                        segs_all.append([(1, 511, 0)])
                    elif i < NB - 1:
                        segs_all.append([(0, 512, t0 - 1)])
                    else:
                        segs_all.append([(0, 512, t0 - 1), (512, 1, SEQ - 1)])
                nseg = sum(len(s) for s in segs_all)
                flat = [s for ss in segs_all for s in ss]
                extra_gen = bias_exp_gen(st) if l == 0 else None
                ring_up = Ring(S, nc, st, "rup", 4 if l == 0 else 6, [8, 256], [wup_b[l, j] for _ in range(NB) for j in range(NJ)], [casts[f"wup{l}"]])
                ring_dn = Ring(S, nc, st, "rdn", 2, [NJ, 128], [wdn_b[l, dc] for _ in range(nseg) for dc in range(8)], [casts[f"wdn{l}"]])
                ring_gp = Ring(S, nc, st, "rgp", 2, [10, 128], [wgp_b[l, dc] for _ in range(nseg) for dc in range(8)], [casts[f"wgp{l}"]])
                ring_p = Ring(S, nc, st, "rp", 2, [2, 512],
                              [(pT_b[l, :, :, tl:tl + n].rearrange("k p n -> p k n"), (lambda d, n=n: d[:, :, 0:n])) for (_, n, tl) in flat],
                              [casts[f"pT{l}"]])
                dF = S.dsem("dF")
                rb = R_BDN0 if l == 0 else R_BDN1
                S.dma("pool", [lambda e: e.dma_start(out=rowF[0:1, :], in_=rows[0:1, rb:rb + 1024])], dF, writes=[trowF])
                for s_ in ("a", "g"):
                    w0, w1, w2 = (colsb[:, COLS[f"w{k}{s_}{l}"]:COLS[f"w{k}{s_}{l}"] + NJ] for k in range(3))
                    bu = colsb[:, COLS[f"bu{s_}{l}"]:COLS[f"bu{s_}{l}"] + NJ]
                    cb = colsb[:, COLS[f"cb{s_}{l}"]:COLS[f"cb{s_}{l}"] + NJ]
                    tmp = dcols[:, DC[f"tmp{l}"]:DC[f"tmp{l}"] + NJ]
                    cK = dcols[:, DC[f"cK{s_}{l}"]:DC[f"cK{s_}{l}"] + NJ]
                    ne0 = dcols[:, DC[f"ne0{s_}{l}"]:DC[f"ne0{s_}{l}"] + NJ]
                    ne2 = dcols[:, DC[f"ne2{s_}{l}"]:DC[f"ne2{s_}{l}"] + NJ]
                    S.op("dve", lambda e, tmp=tmp, w0=w0, w1=w1: e.tensor_tensor(out=tmp, in0=w0, in1=w1, op=ALU.add), reads=[tcols, tdcols], writes=[tdcols])
                    S.op("dve", lambda e, tmp=tmp, w2=w2: e.tensor_tensor(out=tmp, in0=tmp, in1=w2, op=ALU.add), reads=[tcols, tdcols], writes=[tdcols])
                    S.op("dve", lambda e, tmp=tmp, bu=bu: e.tensor_tensor(out=tmp, in0=tmp, in1=bu, op=ALU.mult), reads=[tcols, tdcols], writes=[tdcols])
                    S.op("dve", lambda e, tmp=tmp, cb=cb, cK=cK: e.tensor_tensor(out=cK, in0=tmp, in1=cb, op=ALU.add), reads=[tcols, tdcols], writes=[tdcols])
                    S.op("dve", lambda e, ne0=ne0, w0=w0, bu=bu: e.scalar_tensor_tensor(out=ne0, in0=w0, scalar=-1.0, in1=bu, op0=ALU.mult, op1=ALU.mult),
                         reads=[tcols, tdcols], writes=[tdcols])
                    S.op("dve", lambda e, ne2=ne2, w2=w2, bu=bu: e.scalar_tensor_tensor(out=ne2, in0=w2, scalar=-1.0, in1=bu, op0=ALU.mult, op1=ALU.mult),
                         reads=[tcols, tdcols], writes=[tdcols])
                bg = colsb[:, COLS[f"bgate{l}"]:COLS[f"bgate{l}"] + 8]
                hbg = dcols[:, DC[f"hbg{l}"]:DC[f"hbg{l}"] + 8]
                S.op("dve", lambda e: e.tensor_scalar(out=hbg, in0=bg, scalar1=0.5, scalar2=None, op0=ALU.mult), reads=[tcols, tdcols], writes=[tdcols])
                for k in range(NCB):
                    S.op("dve", lambda e, k=k: e.memset(cag[k], 0.0), writes=[tcag[k]])

                def up_conv(i, bg_gen):
                    t0 = 512 * i
                    txi = tx(t0, t0 + 512)
                    last = (i == NB - 1)
                    hl_r, hl_w = hl[(i + 1) % 2], hl[i % 2]
                    thl_r, thl_w = thl[(i + 1) % 2], thl[i % 2]
                    cy_r, cy_w = carry[(i + 1) % 2], carry[i % 2]
                    tcy_r, tcy_w = tcarry[(i + 1) % 2], tcarry[i % 2]
                    for j in range(NJ):
                        slot = ring_up.get(i * NJ + j)
                        ba, bgk = j % 3, 3 + j % 3
                        pa, pg = bank(ba), bank(bgk)
                        for (pb, off, bk) in ((pa, 0, ba), (pg, 128, bgk)):
                            for kc in range(8):
                                S.op("pe", lambda e, pb=pb, slot=slot, kc=kc, off=off: e.matmul(
                                    pb, lhsT=ring_up.tile[:, slot, kc, off:off + 128], rhs=xs[:, kc, t0:t0 + 512], start=(kc == 0), stop=(kc == 7)),
                                    reads=[ring_up.Ts[slot]] + txi, writes=[tbank[bk]])
                        cb_ = j % NCB
                        c2, tc = cag[cb_], tcag[cb_]
                        for (pb, bk, s_, si) in ((pa, ba, "a", 0), (pg, bgk, "g", 1)):
                            w0, w1 = col(f"w0{s_}{l}", j), col(f"w1{s_}{l}", j)
                            cK = dcol(f"cK{s_}{l}", j)
                            S.op("act", lambda e, c2=c2, si=si, pb=pb, w1=w1, cK=cK: e.activation(out=c2[:, si, 1:513], in_=pb, func=AF.Identity, bias=cK, scale=w1),
                                 reads=[tbank[bk], tcols, tdcols], writes=[tc])
                            if not last:
                                S.op("act", lambda e, pb=pb, si=si, j=j, w0=w0: e.activation(out=hl_w[:, si, j:j + 1], in_=pb[:, 511:512], func=AF.Identity, scale=w0),
                                     reads=[tbank[bk], tcols], writes=[thl_w])
                        if i > 0:
                            S.op("dve", lambda e, c2=c2, j=j: e.tensor_copy(out=c2[:, :, 0], in_=cy_r[:, :, j]), reads=[tcy_r], writes=[tc])
                        for (pb, bk, s_, si) in ((pa, ba, "a", 0), (pg, bgk, "g", 1)):
                            w0 = col(f"w0{s_}{l}", j)
                            S.op("dve", lambda e, c2=c2, si=si, pb=pb, w0=w0: e.scalar_tensor_tensor(out=c2[:, si, 2:513], in0=pb[:, 0:511], scalar=w0, in1=c2[:, si, 2:513],
                                                                                                   op0=ALU.mult, op1=ALU.add),
                                 reads=[tbank[bk], tcols], writes=[tc])
                        if i > 0:
                            S.op("dve", lambda e, c2=c2, j=j: e.tensor_tensor(out=c2[:, :, 1], in0=c2[:, :, 1], in1=hl_r[:, :, j], op=ALU.add),
                                 reads=[thl_r], writes=[tc])
                        else:
                            o0 = DC[f"ne0a{l}"] + j
                            S.op("dve", lambda e, c2=c2, o0=o0: e.tensor_tensor(out=c2[:, :, 1], in0=c2[:, :, 1], in1=dcols[:, o0:o0 + NJ + 1:NJ], op=ALU.add),
                                 reads=[tdcols], writes=[tc])
                        if not last:
                            S.op("dve", lambda e, c2=c2, j=j: e.tensor_copy(out=cy_w[:, :, j], in_=c2[:, :, 512]), reads=[tc], writes=[tcy_w])
                        else:
                            o2 = DC[f"ne2a{l}"] + j
                            S.op("dve", lambda e, c2=c2, o2=o2: e.tensor_tensor(out=c2[:, :, 512], in0=c2[:, :, 512], in1=dcols[:, o2:o2 + NJ + 1:NJ], op=ALU.add),
                                 reads=[tdcols], writes=[tc])
                        for (pb, bk, s_, si) in ((pa, ba, "a", 0), (pg, bgk, "g", 1)):
                            w2 = col(f"w2{s_}{l}", j)
                            S.op("dve", lambda e, c2=c2, si=si, pb=pb, w2=w2: e.scalar_tensor_tensor(out=c2[:, si, 0:512], in0=pb[:, 0:512], scalar=w2, in1=c2[:, si, 0:512],
                                                                                                   op0=ALU.mult, op1=ALU.add),
                                 reads=[tbank[bk], tcols], writes=[tc])

                        def finish(jj):
                            cbp = jj % NCB
                            cp, tcp = cag[cbp], tcag[cbp]
                            gk, tgk = ga[jj % 3], tga[jj % 3]
                            S.op("act", lambda e, gk=gk, cp=cp: e.activation(out=gk[:, 0:513], in_=cp[:, 0, 0:513], func=AF.Gelu_apprx_tanh),
                                 reads=[tcp], writes=[tgk])
                            S.op("dve", lambda e, gk=gk, jj=jj, cp=cp: e.tensor_tensor(out=act[:, jj, 0:513], in0=gk[:, 0:513], in1=cp[:, 1, 0:513], op=ALU.mult),
                                 reads=[tgk, tcp], writes=[tact[jj]])

                        if j > 0:
                            finish(j - 1)
                        if j == NJ - 1:
                            finish(j)
                        if bg_gen is not None:
                            next(bg_gen, None)
                        if extra_gen is not None:
                            next(extra_gen, None)
                    if bg_gen is not None:
                        for _ in bg_gen:
                            pass

                def down(seg_idx, c0, n, tl, bg_gen):
                    txs_seg = tx(tl, tl + n)
                    pre, tpre = pres[seg_idx % 2], tpres[seg_idx % 2]
                    bg_sched = [0, 2, 2, 3, 3, 2, 0, 0]
                    if bg_gen is not None:
                        for _ in range(2):
                            next(bg_gen, None)
                    for dc in range(8):
                        slot = ring_dn.get(seg_idx * 8 + dc)
                        bk = 6 + dc % 2
                        pb = bank(bk)
                        for fc in range(NJ):
                            S.op("pe", lambda e, pb=pb, slot=slot, fc=fc: e.matmul(
                                pb[:, 0:n], lhsT=ring_dn.tile[:, slot, fc, :], rhs=act[:, fc, c0:c0 + n], start=(fc == 0), stop=False),
                                reads=[ring_dn.Ts[slot], tact[fc]], writes=[tbank[bk]])
                        S.op("pe", lambda e, pb=pb, dc=dc: e.matmul(
                            pb[:, 0:n], lhsT=rowF[0:1, dc * 128:dc * 128 + 128], rhs=onesb[0:1, 0:n], start=False, stop=True),
                            reads=[trowF, tones], writes=[tbank[bk]])
                        S.op("dve", lambda e, pb=pb, dc=dc: e.scalar_tensor_tensor(
                            out=pre[:, dc, 0:n], in0=xs[:, dc, tl:tl + n], scalar=ALPHA, in1=pb[:, 0:n], op0=ALU.mult, op1=ALU.add),
                            reads=[tbank[bk]] + txs_seg, writes=[tpre[dc]])
                        if bg_gen is not None:
                            for _ in range(bg_sched[dc]):
                                next(bg_gen, None)

                def tail_gen(seg_idx, c0, n, tl):
                    txs_seg = tx(tl, tl + n)
                    pre, tpre = pres[seg_idx % 2], tpres[seg_idx % 2]
                    pslot = ring_p.get(seg_idx)

                    def affine(dc):
                        S.op("act", lambda e, dc=dc: e.activation(out=pre[:, dc, 0:n], in_=pre[:, dc, 0:n], func=AF.Identity,
                                                                  bias=col(f"ln2b{l}", dc), scale=col(f"ln2g{l}", dc)),
                             reads=[tpre[dc], tcols], writes=[tpre[dc]])
                        S.op("act", lambda e, dc=dc: e.activation(out=xs[:, dc, tl:tl + n], in_=pre[:, dc, 0:n], func=AF.Copy),
                             reads=[tpre[dc]], writes=txs_seg)

                    yield from layer_norm_gen(pre, n, tpre, prebf, sq, tcag[0:2], tcag[2:4], lt, affine, banks=(0, 1), norm_eng=("dve", "dve"))
                    for dc in range(8):
                        slot = ring_gp.get(seg_idx * 8 + dc)
                        bgt, bpj = 6, 7
                        pgt, ppj = bank(bgt), bank(bpj)
                        for kc in range(8):
                            S.op("pe", lambda e, pgt=pgt, slot=slot, kc=kc: e.matmul(
                                pgt[:, 0:n], lhsT=ring_gp.tile[:, slot, kc, :], rhs=xs[:, kc, tl:tl + n], start=(kc == 0), stop=(kc == 7)),
                                reads=[ring_gp.Ts[slot]] + txs_seg, writes=[tbank[bgt]])
                        for kk in range(2):
                            S.op("pe", lambda e, ppj=ppj, slot=slot, kk=kk: e.matmul(
                                ppj[:, 0:n], lhsT=ring_gp.tile[:, slot, 8 + kk, :], rhs=ring_p.tile[:, pslot, kk, 0:n], start=(kk == 0), stop=(kk == 1)),
                                reads=[ring_gp.Ts[slot], ring_p.Ts[pslot]], writes=[tbank[bpj]])
                        sgk = sg[dc % 2]
                        S.op("act", lambda e, sgk=sgk, pgt=pgt, dc=dc: e.activation(out=sgk[:, 0:n], in_=pgt[:, 0:n], func=AF.Tanh,
                                                                                    bias=dcol(f"hbg{l}", dc), scale=0.5),
                             reads=[tbank[bgt], tdcols], writes=[tsg[dc % 2]])
                        S.op("dve", lambda e, sgk=sgk, ppj=ppj: e.scalar_tensor_tensor(out=sgk[:, 0:n], in0=sgk[:, 0:n], scalar=1.0, in1=ppj[:, 0:n],
                                                                                      op0=ALU.add, op1=ALU.mult),
                             reads=[tbank[bpj]], writes=[tsg[dc % 2]])
                        S.op("dve", lambda e, sgk=sgk, dc=dc: e.scalar_tensor_tensor(out=pre[:, dc, 0:n], in0=sgk[:, 0:n], scalar=0.5, in1=pre[:, dc, 0:n],
                                                                                    op0=ALU.mult, op1=ALU.add),
                             reads=[tsg[dc % 2]], writes=[tpre[dc]])
                        yield
                    if final:
                        dO = dOut[seg_idx % 2]
                        S.dma("sp", [lambda e: e.dma_start(out=outT[:, :, tl:tl + n].rearrange("d p n -> p d n"), in_=pre[:, :, 0:n],
                                                           allow_slow_non_contiguous=True)],
                              dO, reads=tpre)
                    else:
                        for dc in range(8):
                            S.op("act", lambda e, dc=dc: e.activation(out=xs[:, dc, tl:tl + n], in_=pre[:, dc, 0:n], func=AF.Copy),
                                 reads=[tpre[dc]], writes=txs_seg)
                            if dc % 2 == 1:
                                yield

                genA = None
                genB = None
                seg_idx = 0
                for i in range(NB):
                    up_conv(i, genA)
                    genA = None
                    for (c0, n, tl) in segs_all[i]:
                        if genA is not None:
                            for _ in genA:
                                pass
                        down(seg_idx, c0, n, tl, genB)
                        genA = genB
                        genB = tail_gen(seg_idx, c0, n, tl)
                        seg_idx += 1
                for g_ in (genA, genB):
                    if g_ is not None:
                        for _ in g_:
                            pass
                if extra_gen is not None:
                    for _ in extra_gen:
                        pass
                S.barrier()


        def window(m):
            if m == 0:
                return [0, 1, 2, 3], [0, 1, 2, 3]
            if m == 1:
                return [0, 1, 2, 3], [4, 5, 6, 7]
            if m == 30:
                return [28, 29, 30, 31], [0, 1, 2, 3]
            if m == 31:
                return [28, 29, 30, 31], [4, 5, 6, 7]
            return [m - 2, m - 1, m, m + 1, m + 2], [0, 1, 2, 3, 4]

        def phase_B():
            NTB = 16
            with ExitStack() as st:
                Kr = sb("Kr", [128, 8, 1024], BF16, st)
                Vr = sb("Vr", [128, 8, 16, 65], BF16, st)
                Qz = [sb(f"Qz{k}", [128, 8, 256], BF16, st) for k in range(2)]
                E = [sb(f"E{k}", [128, 640], BF16, st) for k in range(3)]
                otok = sb("otok", [128, 2, 1024], BF16, st)
                oT = sb("oT", [128, 8, 256], BF16, st)
                rowB = sb("rowB", [1, 2048], BF16, st)
                pre = sb("preB", [128, 8, 256], F32, st)
                prebf = sb("prebfB", [128, 8, 256], BF16, st)
                sq = sb("sqB", [128, 8, 256], BF16, st)
                rc = [sb(f"rc{k}", [128, 1], F32, st) for k in range(2)]
                lt = alloc_ln_temps(st, 256)
                tK = [T(f"K{k}") for k in range(4)]
                tV = [T(f"V{k}") for k in range(8)]
                tQ, tE, totok, toT, trowB, tprebf, tsq = T(), [T(), T(), T()], [T(), T()], T(), T(), T(), T()
                tpre = [T() for _ in range(8)]
                trc = [T(), T()]
                qk_srcs, v_srcs = [], []
                order = [("kv", 0)]
                for ti in range(NTB):
                    if ti + 1 < NTB:
                        order.append(("kv", ti + 1))
                    order.append(("q", ti))
                for (kind, _) in order:
                    if kind == "kv":
                        qk_srcs += [wqk_b[8 + fc] for fc in range(8)]
                        v_srcs += [wvn_b[vb] for vb in range(2)]
                    else:
                        qk_srcs += [wqk_b[fc] for fc in range(8)]
                ring_qk = Ring(S, nc, st, "rqk", 4, [8, 128], qk_srcs, [casts["wqk"]])
                ring_v = Ring(S, nc, st, "rv", 2, [8, 512], v_srcs, [casts["wvn"]])
                ring_won = Ring(S, nc, st, "rwon", 3, [8, 128], [won_b[dc] for _ in range(NTB) for dc in range(8)], [casts["won"]])
                bias_srcs = []
                for ti in (0, NTB - 1):
                    lo, nt = (5, 8) if ti == 0 else (13, 8)
                    for h in range(16):
                        bias_srcs.append((rpbt_b[h, :, lo:lo + nt, :], (lambda d, nt=nt: d[:, 0:nt, :])))
                bias_res = sb("bias_res", [128, 16, 5, 128], BF16, st)
                tbres = T("bias_res")
                dbres = S.dsem("dbres")
                S.dma("sp", [lambda e, h=h: e.dma_start(out=bias_res[:, h], in_=rpbt_b[h, :, 0:5, :]) for h in range(16)], dbres,
                      reads=[t_rpscr], writes=[tbres])
                ring_b = Ring(S, nc, st, "rb", 3, [8, 128], bias_srcs, [t_rpscr])
                dB = S.dsem("dB")
                S.dma("pool", [lambda e: e.dma_start(out=rowB[0:1, 0:1024], in_=rows[0:1, R_BV:R_BV + 1024]),
                               lambda e: e.dma_start(out=rowB[0:1, 1024:2048], in_=rows[0:1, R_BO:R_BO + 1024])], dB, writes=[trowB])
                S.op("dve", lambda e: e.memset(Vr[:, :, :, 64:65], 1.0), writes=tV)
                S.op("dve", lambda e: e.memset(Qz[0][64:128], 0.0), writes=[tQ])
                S.op("dve", lambda e: e.memset(Qz[1][0:64], 0.0), writes=[tQ])
                cnt = {"qk": 0, "v": 0}

                bgB = [None]

                def step_bg(k=1):
                    if bgB[0] is not None:
                        for _ in range(k):
                            next(bgB[0], None)

                def kv_proj(ti):
                    tk0 = 256 * ti
                    txi = tx(tk0, tk0 + 256)
                    ks = ti % 4
                    for fc in range(8):
                        slot = ring_qk.get(cnt["qk"])
                        cnt["qk"] += 1
                        pb = bank(fc % 2)
                        for kc in range(8):
                            S.op("pe", lambda e, pb=pb, slot=slot, kc=kc: e.matmul(
                                pb[:, 0:256], lhsT=ring_qk.tile[:, slot, kc, :], rhs=xs[:, kc, tk0:tk0 + 256], start=(kc == 0), stop=(kc == 7)),
                                reads=[ring_qk.Ts[slot]] + txi, writes=[tbank[fc % 2]])
                        S.op("act", lambda e, pb=pb, fc=fc: e.activation(out=Kr[:, fc, ks * 256:ks * 256 + 256], in_=pb[:, 0:256], func=AF.Identity,
                                                                          bias=col("bk", fc), scale=1.0),
                             reads=[tbank[fc % 2], tcols], writes=[tK[ks]])
                        step_bg()
                    for vb in range(2):
                        slot = ring_v.get(cnt["v"])
                        cnt["v"] += 1
                        for blk in range(2):
                            tb0 = tk0 + 128 * blk
                            vs = (2 * ti + blk) % 8
                            bk = 2 + (vb * 2 + blk) % 2
                            pb = bank(bk)
                            for kc in range(8):
                                S.op("pe", lambda e, pb=pb, slot=slot, kc=kc, tb0=tb0: e.matmul(
                                    pb, lhsT=xs[:, kc, tb0:tb0 + 128], rhs=ring_v.tile[:, slot, kc, :], start=(kc == 0), stop=False),
                                    reads=[ring_v.Ts[slot]] + txi, writes=[tbank[bk]])
                            S.op("pe", lambda e, pb=pb, vb=vb: e.matmul(
                                pb, lhsT=onesb[0:1, 0:128], rhs=rowB[0:1, vb * 512:vb * 512 + 512], start=False, stop=True),
                                reads=[tones, trowB], writes=[tbank[bk]])
                            S.op("dve", lambda e, pb=pb, vs=vs, vb=vb: e.tensor_copy(
                                out=Vr[:, vs, vb * 8:vb * 8 + 8, 0:64], in_=pb.rearrange("p (h d) -> p h d", h=8)),
                                reads=[tbank[bk]], writes=[tV[vs]])
                            step_bg()

                def q_proj(ti):
                    tk0 = 256 * ti
                    txi = tx(tk0, tk0 + 256)
                    for fc in range(8):
                        slot = ring_qk.get(cnt["qk"])
                        cnt["qk"] += 1
                        pb = bank(fc % 2)
                        for kc in range(8):
                            S.op("pe", lambda e, pb=pb, slot=slot, kc=kc: e.matmul(
                                pb[:, 0:256], lhsT=ring_qk.tile[:, slot, kc, :], rhs=xs[:, kc, tk0:tk0 + 256], start=(kc == 0), stop=(kc == 7)),
                                reads=[ring_qk.Ts[slot]] + txi, writes=[tbank[fc % 2]])
                        for k_ in range(2):
                            S.op("dve", lambda e, pb=pb, fc=fc, k_=k_: e.tensor_scalar(
                                out=Qz[k_][k_ * 64:k_ * 64 + 64, fc, :], in0=pb[k_ * 64:k_ * 64 + 64, 0:256],
                                scalar1=colsb[k_ * 64:k_ * 64 + 64, COLS["bq"] + fc:COLS["bq"] + fc + 1], scalar2=0.125,
                                op0=ALU.add, op1=ALU.mult),
                                reads=[tbank[fc % 2], tcols], writes=[tQ])
                        step_bg()

                kv_proj(0)
                unit = 0
                for ti in range(NTB):
                    tk0 = 256 * ti
                    txi = tx(tk0, tk0 + 256)
                    if ti + 1 < NTB:
                        kv_proj(ti + 1)
                    q_proj(ti)
                    if bgB[0] is not None:
                        for _ in bgB[0]:
                            pass
                        bgB[0] = None

                    def qk_exp(h, pr, bslot, u):
                        fc, hp = h // 2, (h % 2) * 64
                        m = 2 * ti + pr
                        blocks, types = window(m)
                        nb = len(blocks)
                        sbk = 1 + u % 3
                        Sb = P2[sbk]
                        tS = [tbank[2 * sbk], tbank[2 * sbk + 1]]
                        Ek, tEk = E[u % 3], tE[u % 3]
                        for jb, (b, ty) in enumerate(zip(blocks, types)):
                            S.op("pe", lambda e, Sb=Sb, jb=jb, b=b: e.matmul(
                                Sb[:, jb * 128:jb * 128 + 128], lhsT=Kr[:, fc, (b % 8) * 128:(b % 8) * 128 + 128],
                                rhs=Qz[h % 2][:, fc, pr * 128:pr * 128 + 128], start=True, stop=True),
                                reads=[tK[(b // 2) % 4], tQ], writes=tS)
                        S.op("act", lambda e, Sb=Sb, Ek=Ek, nb=nb: e.activation(out=Ek[:, 0:nb * 128], in_=Sb[:, 0:nb * 128], func=AF.Exp),
                             reads=tS, writes=[tEk])
                        ty0 = types[0]
                        if bslot is None:
                            btile, tbt = bias_res[:, h, ty0:ty0 + nb, :], tbres
                        else:
                            btile, tbt = ring_b.tile[:, bslot, ty0:ty0 + nb, :], ring_b.Ts[bslot]
                        S.op("dve", lambda e, Ek=Ek, nb=nb, btile=btile: e.tensor_tensor(
                            out=Ek[:, 0:nb * 128], in0=Ek[:, 0:nb * 128],
                            in1=btile.rearrange("p a b -> p (a b)"), op=ALU.mult),
                            reads=[tbt], writes=[tEk])

                    def pv_norm(h, pr, u):
                        m = 2 * ti + pr
                        blocks, _ = window(m)
                        nb = len(blocks)
                        Ek, tEk = E[u % 3], tE[u % 3]
                        ob = u % 2
                        Ob = bank(ob)
                        for jb, b in enumerate(blocks):
                            S.op("pe", lambda e, Ob=Ob, Ek=Ek, jb=jb, b=b: e.matmul(
                                Ob[:, 0:65], lhsT=Ek[:, jb * 128:jb * 128 + 128], rhs=Vr[:, b % 8, h, :], start=(jb == 0), stop=(jb == nb - 1)),
                                reads=[tEk, tV[b % 8]], writes=[tbank[ob]])
                        S.op("dve", lambda e, Ob=Ob, ob=ob: e.reciprocal(out=rc[ob][:], in_=Ob[:, 64:65]), reads=[tbank[ob]], writes=[trc[ob]])
                        S.op("dve", lambda e, Ob=Ob, ob=ob: e.tensor_scalar(out=otok[:, pr, h * 64:h * 64 + 64], in0=Ob[:, 0:64], scalar1=rc[ob][:, 0:1],
                                                                            scalar2=None, op0=ALU.mult),
                             reads=[tbank[ob], trc[ob]], writes=[totok[pr]])

                    prevs = []
                    ring_won.ahead(ti * 8)
                    ring_qk.ahead(cnt["qk"])
                    ring_v.ahead(cnt["v"])
                    for h in range(16):
                        if ti == 0:
                            bslot = ring_b.get(h)
                        elif ti == NTB - 1:
                            bslot = ring_b.get(16 + h)
                        else:
                            bslot = None
                        for pr in range(2):
                            qk_exp(h, pr, bslot, unit)
                            prevs.append((h, pr, unit))
                            if len(prevs) > 2:
                                pv_norm(*prevs.pop(0))
                            unit += 1
                    while prevs:
                        pv_norm(*prevs.pop(0))
                    for pr in range(2):
                        tb = bank(2 + pr).bitcast(BF16)
                        for fc in range(8):
                            S.op("pe", lambda e, tb=tb, fc=fc: e.transpose(tb[:, fc * 128:fc * 128 + 128], otok[:, pr, fc * 128:fc * 128 + 128], identb[:]),
                                 reads=[totok[pr], tident], writes=[tbank[2 + pr]])
                        S.op("act", lambda e, tb=tb: e.activation(out=oT[:, :, pr * 128:pr * 128 + 128], in_=tb.rearrange("p (f q) -> p f q", f=8), func=AF.Copy),
                             reads=[tbank[2 + pr]], writes=[toT])
                    for dc in range(8):
                        slot = ring_won.get(ti * 8 + dc)
                        pb = bank(dc % 2)
                        for fc in range(8):
                            S.op("pe", lambda e, pb=pb, slot=slot, fc=fc: e.matmul(
                                pb[:, 0:256], lhsT=ring_won.tile[:, slot, fc, :], rhs=oT[:, fc, :], start=(fc == 0), stop=False),
                                reads=[ring_won.Ts[slot], toT], writes=[tbank[dc % 2]])
                        S.op("pe", lambda e, pb=pb, dc=dc: e.matmul(
                            pb[:, 0:256], lhsT=rowB[0:1, 1024 + dc * 128:1024 + dc * 128 + 128], rhs=onesb[0:1, 0:256], start=False, stop=True),
                            reads=[trowB, tones], writes=[tbank[dc % 2]])
                        S.op("dve", lambda e, pb=pb, dc=dc: e.scalar_tensor_tensor(
                            out=pre[:, dc, :], in0=xs[:, dc, tk0:tk0 + 256], scalar=ALPHA, in1=pb[:, 0:256], op0=ALU.mult, op1=ALU.add),
                            reads=[tbank[dc % 2]] + txi, writes=[tpre[dc]])

                    def affine(dc, tk0=tk0, txi=txi):
                        S.op("act", lambda e, dc=dc: e.activation(out=xs[:, dc, tk0:tk0 + 256], in_=pre[:, dc, :], func=AF.Identity,
                                                                  bias=col("ln1b1", dc), scale=col("ln1g1", dc)),
                             reads=[tpre[dc], tcols], writes=txi)

                    bgB[0] = layer_norm_gen(pre, 256, tpre, prebf, sq, [tprebf], [tsq], lt, affine)
                for _ in bgB[0]:
                    pass
                S.barrier()

        dOut = [S.dsem("dOut0"), S.dsem("dOut1")]
        phases = [("A", phase_A), ("F0", lambda: phase_F(0, stop == "F0" and False)), ("B", phase_B), ("F1", lambda: phase_F(1, True))]
        for name, fn in phases:
            fn()
            if stop == name:
                break
        if stop is not None and stop != "F1":
            dD = S.dsem("dDump")
            S.barrier()
            S.dma("pool", [lambda e, kc=kc: e.dma_start(out=outT[kc], in_=xs[:, kc, :]) for kc in range(8)], dD, reads=txs)
        S.barrier()
        out_plan = S.get_plan()
        if os.environ.get("KVERBOSE"):
            print("ops", S.cnt, "incs", {e: len(v) for e, v in out_plan.items()}, "nsem", S.nsem)
    return nc, out_plan


def _colv(v, n):
    return np.ascontiguousarray(np.asarray(v, np.float32).reshape(n, 128).T)


def _lhsT_tiles(W, kchunks, ochunks):
    K, O = W.shape
    return np.ascontiguousarray(W.reshape(kchunks, 128, ochunks, 128).transpose(2, 1, 0, 3))


def _rhs_tiles(W, kchunks):
    K, N = W.shape
    return np.ascontiguousarray(W.reshape(kchunks, 128, N).transpose(1, 0, 2))


def _bias_tiles(rpb):
    specs = [(10, 10 + j) for j in (-2, -1, 0, 1, 2)]
    specs += [(0, b) for b in range(4)] + [(1, b) for b in range(4)]
    specs += [(30, b) for b in range(28, 32)] + [(31, b) for b in range(28, 32)]
    out = np.full((16, 128, 21, 128), NEG, np.float32)
    kc = np.arange(64)
    qc = np.arange(64)
    cs = np.clip(qc - 8, 0, 48)
    colvalid = (kc[:, None] >= cs[None, :]) & (kc[:, None] < cs[None, :] + 16)
    dcidx = np.clip(kc[:, None] - qc[None, :], -15, 15) + 15
    for ty, (m, b) in enumerate(specs):
        for pk in range(2):
            for pq in range(2):
                kr = 2 * b + pk
                r = 2 * m + pq
                kr0 = min(max(r - 4, 0), 56)
                if not (kr0 <= kr < kr0 + 8):
                    continue
                vals = rpb[:, kr - r + 7, :][:, dcidx]
                blk = np.where(colvalid[None], vals, np.float32(NEG))
                out[:, pk * 64:pk * 64 + 64, ty, pq * 64:pq * 64 + 64] = blk
    return out


def _prep_shared(inp):
    f = lambda k: np.asarray(inp[k], np.float32)
    sh = {}
    w_in = f("sg_w_in")[0]
    sh["wu"] = _lhsT_tiles(w_in[:, :2048], 8, 16)
    sh["wv"] = _rhs_tiles(w_in[:, 2048:], 8)
    sh["wso"] = _lhsT_tiles(f("sg_w_out")[0], 16, 8)
    wup = f("ffn_w_up")
    ups = []
    for l in range(2):
        a = wup[l][:, :FF].reshape(8, 128, NJ, 128).transpose(2, 1, 0, 3)
        g = wup[l][:, FF:].reshape(8, 128, NJ, 128).transpose(2, 1, 0, 3)
        ups.append(np.concatenate([a, g], axis=3))
    sh["wup"] = np.ascontiguousarray(np.stack(ups))
    sh["wdn"] = np.ascontiguousarray(np.stack([_lhsT_tiles(f("ffn_w_down")[l], NJ, 8) for l in range(2)]))
    gp = []
    for l in range(2):
        g = _lhsT_tiles(f("ple_w_gate")[l], 8, 8)
        p = _lhsT_tiles(f("ple_w_proj")[l], 2, 8)
        gp.append(np.concatenate([g, p], axis=2))
    sh["wgp"] = np.ascontiguousarray(np.stack(gp))
    wqkv = f("na_w_qkv")[0]
    sh["wqk"] = _lhsT_tiles(wqkv[:, :2048], 8, 16)
    sh["wvn"] = np.ascontiguousarray(np.stack([_rhs_tiles(wqkv[:, 2048 + vb * 512:2048 + vb * 512 + 512], 8) for vb in range(2)]))
    sh["won"] = _lhsT_tiles(f("na_w_o")[0], 8, 8)
    sh["wst"] = np.ascontiguousarray(f("sg_w_s")[0].transpose(2, 0, 1))
    cols = np.zeros((128, NCOL), np.float32)

    def put(name, v, n):
        cols[:, COLS[name]:COLS[name] + n] = _colv(v, n)

    put("bu_in", f("sg_b_in")[0][:2048], 16)
    put("lng", f("sg_ln_g")[0], 16)
    for l in range(2):
        put(f"ln1g{l}", f("ln1_g")[l], 8)
        put(f"ln1b{l}", f("ln1_b")[l], 8)
        put(f"ln2g{l}", f("ln2_g")[l], 8)
        put(f"ln2b{l}", f("ln2_b")[l], 8)
        put(f"bgate{l}", f("ple_b_gate")[l], 8)
        put(f"bua{l}", f("ffn_b_up")[l][:FF], NJ)
        put(f"bug{l}", f("ffn_b_up")[l][FF:], NJ)
        for k in range(3):
            put(f"w{k}a{l}", f("ffn_conv_w")[l][k][:FF], NJ)
            put(f"w{k}g{l}", f("ffn_conv_w")[l][k][FF:], NJ)
        put(f"cba{l}", f("ffn_conv_b")[l][:FF], NJ)
        put(f"cbg{l}", f("ffn_conv_b")[l][FF:], NJ)
    bqkv = f("na_b_qkv")[0]
    put("bq", bqkv[:1024], 8)
    put("bk", bqkv[1024:2048], 8)
    sh["cols"] = cols
    rows = np.zeros((1, NROW), np.float32)
    rows[0, R_BINV:R_BINV + 2048] = f("sg_b_in")[0][2048:]
    rows[0, R_BV:R_BV + 1024] = bqkv[2048:]
    rows[0, R_BOUT:R_BOUT + 1024] = f("sg_b_out")[0]
    rows[0, R_BDN0:R_BDN0 + 1024] = f("ffn_b_down")[0]
    rows[0, R_BDN1:R_BDN1 + 1024] = f("ffn_b_down")[1]
    rows[0, R_BO:R_BO + 1024] = f("na_b_o")[0]
    rows[0, R_BS:R_BS + 1024] = f("sg_b_s")[0].reshape(-1)
    rows[0, R_LNB:R_LNB + 2048] = f("sg_ln_b")[0]
    sh["rows"] = rows
    sh["rpbt"] = _bias_tiles(f("na_rpb")[0])
    sh["ident"] = np.eye(128, dtype=np.float32)
    return sh


def kernel(**inputs):
    stop = os.environ.get("KSTOP") or None
    ncores = int(os.environ.get("KCORES", "8"))
    sh = _prep_shared(inputs)
    x = np.asarray(inputs["x"], np.float32)
    p = np.asarray(inputs["p"], np.float32)
    in_maps = []
    for b in range(ncores):
        m = dict(sh)
        m["xT"] = np.ascontiguousarray(x[b].T).reshape(8, 128, SEQ)
        m["pT"] = np.ascontiguousarray(p[:, b].transpose(0, 2, 1)).reshape(2, 2, 128, SEQ)
        in_maps.append(m)
    nc = build_program(stop)
    res = run_bass_kernel_spmd(nc, in_maps, core_ids=list(range(ncores)))
    out = np.empty((ncores, SEQ, D), np.float32)
    for b in range(ncores):
        out[b] = res.results[b]["outT"].reshape(D, SEQ).T
    return out
```

```python
import os
import numpy as np
import concourse.bass as bass
import concourse.mybir as mybir
from concourse.bass_utils import run_bass_kernel_spmd
from contextlib import ExitStack

F32 = mybir.dt.float32
BF16 = mybir.dt.bfloat16
AF = mybir.ActivationFunctionType
ALU = mybir.AluOpType

D = 1024
SEQ = 4096
NB = 8
FF = 2816
NJ = 22
ALPHA = float(4.0 ** 0.25)
EPS = 1e-5
NEG = -30000.0
EPOCH = 4096

COLS = {}


def _mkcols():
    off = 0

    def add(name, n):
        nonlocal off
        COLS[name] = off
        off += n

    add("bu_in", 16)
    add("lng", 16)
    for l in (0, 1):
        for nm in ("ln1g", "ln1b", "ln2g", "ln2b", "bgate"):
            add(f"{nm}{l}", 8)
        for nm in ("bua", "bug", "w0a", "w1a", "w2a", "w0g", "w1g", "w2g", "cba", "cbg"):
            add(f"{nm}{l}", NJ)
    add("bq", 8)
    add("bk", 8)
    return off


NCOL = _mkcols()
DC = {}


def _mkd():
    off = 0

    def add(name, n):
        nonlocal off
        DC[name] = off
        off += n

    add("eps", 1)
    add("ones", 128)
    for l in (0, 1):
        for nm in ("cKa", "cKg", "ne0a", "ne0g", "ne2a", "ne2g", "tmp"):
            add(f"{nm}{l}", NJ)
        add(f"hbg{l}", 8)
    return off


NDC = _mkd()
R_BINV, R_BV, R_BOUT, R_BDN0, R_BDN1, R_BO, R_BS, R_LNB = 0, 2048, 3072, 4096, 5120, 6144, 7168, 8192
NROW = 10240


class T:
    __slots__ = ("name", "w", "r")

    def __init__(self, name=""):
        self.name = name
        self.w = None
        self.r = []


class Sched:
    ENGS = ("pe", "act", "dve", "pool", "sp")

    def __init__(self, nc, es, plan=None):
        import bisect
        self._bisect = bisect
        self.nc = nc
        self.es = es
        self.plan = plan
        self.cnt = {e: 0 for e in self.ENGS}
        self.inc = {e: 0 for e in self.ENGS}
        self.sems = {e: [] for e in self.ENGS}
        self.waited = {e: {} for e in self.ENGS}
        self.last = {e: None for e in self.ENGS}
        self.needed = {e: set() for e in self.ENGS}
        self.dma_toks = {}
        self.nsem = 0
        self.eng = {"pe": nc.tensor, "act": nc.scalar, "dve": nc.vector, "pool": nc.gpsimd, "sp": nc.sync}

    def new_sem(self, name):
        self.nsem += 1
        return self.es.enter_context(self.nc.semaphore(f"{name}_{self.nsem}"))

    def dsem(self, name):
        return [self.new_sem(name), 0]

    def _eng_sem(self, e, k):
        while len(self.sems[e]) <= k:
            self.sems[e].append(self.new_sem(f"s_{e}_{len(self.sems[e])}"))
        return self.sems[e][k]

    @staticmethod
    def _deps(reads, writes):
        deps = []
        for t in reads:
            if t.w is not None:
                deps.append(t.w)
        for t in writes:
            if t.w is not None:
                deps.append(t.w)
            deps.extend(t.r)
        return deps

    def _value(self, src, idx):
        if self.plan is None:
            return idx + 1
        return self._bisect.bisect_right(self.plan[src], idx)

    def _emit_waits(self, e, deps):
        best = {}
        for tok in deps:
            if tok[0] == "dma":
                key = ("dma", id(tok[1]))
                if key not in best or best[key][2] < tok[2]:
                    best[key] = tok
            else:
                src, idx = tok
                if src == "pe" and e == "pe":
                    continue
                if src not in best or best[src][1] < idx:
                    best[src] = tok
        wd = self.waited[e]
        for key, tok in best.items():
            v = tok[2] if tok[0] == "dma" else tok[1]
            if wd.get(key, -1) >= v:
                continue
            wd[key] = v
            if tok[0] == "dma":
                self.eng[e].wait_ge(tok[1], tok[2])
            else:
                src, idx = tok
                self.needed[src].add(idx)
                val = self._value(src, idx)
                self.eng[e].wait_ge(self._eng_sem(src, (val - 1) // EPOCH), (val - 1) % EPOCH + 1)

    @staticmethod
    def _mark(tok, reads, writes):
        for t in reads:
            t.r.append(tok)
        for t in writes:
            t.w = tok
            t.r = []

    def op(self, e, fn, reads=(), writes=()):
        self._emit_waits(e, self._deps(reads, writes))
        i = self.cnt[e]
        self.cnt[e] += 1
        ins = fn(self.eng[e])
        if self.plan is None:
            ins.then_inc(self._eng_sem(e, i // EPOCH), 1)
        else:
            pl = self.plan[e]
            k = self._bisect.bisect_left(pl, i)
            if k < len(pl) and pl[k] == i:
                self.inc[e] += 1
                v = self.inc[e]
                assert v == k + 1
                ins.then_inc(self._eng_sem(e, (v - 1) // EPOCH), 1)
        tok = (e, i)
        self.last[e] = tok
        self._mark(tok, reads, writes)
        return tok

    def dma(self, e, fns, ds, reads=(), writes=(), track=True):
        self._emit_waits(e, self._deps(reads, writes))
        for fn in fns:
            ds[1] += 16
            fn(self.eng[e]).then_inc(ds[0], 16)
        tok = ("dma", ds[0], ds[1])
        if track:
            self.dma_toks[id(ds[0])] = tok
        self._mark(tok, reads, writes)
        return tok

    def barrier(self):
        toks = [t for t in self.last.values() if t is not None] + list(self.dma_toks.values())
        for e in self.ENGS:
            self._emit_waits(e, toks)

    def get_plan(self):
        return {e: sorted(self.needed[e]) for e in self.ENGS}


class Ring:
    uid = 0

    def __init__(self, S, nc, es, name, nslots, shape, srcs, src_T):
        self.S = S
        self.n = nslots
        Ring.uid += 1
        name = f"{name}_{Ring.uid}"
        self.tile = es.enter_context(nc.sbuf_tensor(name, [128, nslots] + list(shape), BF16))
        self.Ts = [T(f"{name}{k}") for k in range(nslots)]
        self.ds = [S.dsem(f"d_{name}{k}") for k in range(nslots)]
        self.srcs = srcs
        self.src_T = src_T
        self.issued = 0

    def ahead(self, k):
        if k < len(self.srcs):
            self.get(k)

    def get(self, k):
        while self.issued < len(self.srcs) and self.issued <= k + self.n - 1:
            j = self.issued
            slot = j % self.n
            src = self.srcs[j]
            dst = self.tile[:, slot]
            if isinstance(src, tuple):
                src, sub = src
                dst = sub(dst)
            self.S.dma("sp", [lambda e, dst=dst, src=src: e.dma_start(out=dst, in_=src, allow_slow_non_contiguous=True)], self.ds[slot],
                       reads=self.src_T, writes=[self.Ts[slot]])
            self.issued += 1
        return k % self.n


def build_program(stop=None):
    _, plan = _build_once(stop, None)
    nc, _ = _build_once(stop, plan)
    return nc


def _build_once(stop, plan):
    nc = bass.Bass("TRN2", target_bir_lowering=False)

    def dram(name, shape, dtype=F32, kind="ExternalInput"):
        return nc.dram_tensor(name, list(shape), dtype, kind=kind).ap()

    xT = dram("xT", [8, 128, SEQ])
    pT = dram("pT", [2, 2, 128, SEQ])
    wu = dram("wu", [16, 128, 8, 128])
    wv = dram("wv", [128, 8, 2048])
    wso = dram("wso", [8, 128, 16, 128])
    wup = dram("wup", [2, NJ, 128, 8, 256])
    wdn = dram("wdn", [2, 8, 128, NJ, 128])
    wgp = dram("wgp", [2, 8, 128, 10, 128])
    wqk = dram("wqk", [16, 128, 8, 128])
    wvn = dram("wvn", [2, 128, 8, 512])
    won = dram("won", [8, 128, 8, 128])
    wst = dram("wst", [128, 8, 128])
    cols = dram("cols", [128, NCOL])
    rows = dram("rows", [1, NROW])
    rpbt = dram("rpbt", [16, 128, 21, 128])
    ident = dram("ident", [128, 128])
    outT = dram("outT", [8, 128, SEQ], kind="ExternalOutput")
    wu_b = dram("wu_b", [16, 128, 8, 128], BF16, "Internal")
    wso_b = dram("wso_b", [8, 128, 16, 128], BF16, "Internal")
    wup_b = dram("wup_b", [2, NJ, 128, 8, 256], BF16, "Internal")
    wdn_b = dram("wdn_b", [2, 8, 128, NJ, 128], BF16, "Internal")
    wgp_b = dram("wgp_b", [2, 8, 128, 10, 128], BF16, "Internal")
    wqk_b = dram("wqk_b", [16, 128, 8, 128], BF16, "Internal")
    wvn_b = dram("wvn_b", [2, 128, 8, 512], BF16, "Internal")
    won_b = dram("won_b", [8, 128, 8, 128], BF16, "Internal")
    rpbt_b = dram("rpbt_b", [16, 128, 21, 128], BF16, "Internal")
    pT_b = dram("pT_b", [2, 2, 128, SEQ], BF16, "Internal")

    with ExitStack() as es:
        S = Sched(nc, es, plan)

        _uid = [0]

        def sb(name, shape, dtype, st=es):
            _uid[0] += 1
            return st.enter_context(nc.sbuf_tensor(f"{name}_{_uid[0]}", list(shape), dtype))

        xs = sb("xs", [128, 8, SEQ], BF16)
        colsb = sb("colsb", [128, NCOL], F32)
        dcols = sb("dcols", [128, NDC], F32)
        identb = sb("identb", [128, 128], BF16)
        onesb = sb("onesb", [128, 512], BF16)
        mmb = sb("mmb", [128, 128], BF16)
        P2 = [es.enter_context(nc.psum_tensor(f"P2_{k}", [128, 1024], F32)) for k in range(4)]

        def bank(k):
            return P2[k // 2][:, (k % 2) * 512:(k % 2) * 512 + 512]

        tbank = [T(f"bank{k}") for k in range(8)]
        txs = [T(f"xs{k}") for k in range(16)]

        def tx(lo, hi):
            return txs[lo // 256:(hi - 1) // 256 + 1]

        tcols, tdcols, tident, tones, tmm = T("cols"), T("dcols"), T("ident"), T("ones"), T("mm")

        def col(name, j=0):
            o = COLS[name] + j
            return colsb[:, o:o + 1]

        def dcol(name, j=0):
            o = DC[name] + j
            return dcols[:, o:o + 1]

        d_misc = S.dsem("d_misc")
        S.dma("sp", [lambda e: e.dma_start(out=colsb[:], in_=cols)], d_misc, writes=[tcols])
        d_x = S.dsem("d_x")
        S.dma("pool", [lambda e, kc=kc: e.dma_start(out=xs[:, kc, :], in_=xT[kc]) for kc in range(8)], d_x, writes=txs)
        d_id = S.dsem("d_id")
        S.dma("pool", [lambda e: e.dma_start(out=identb[:], in_=ident)], d_id, writes=[tident])
        S.op("dve", lambda e: e.memset(onesb[:], 1.0), writes=[tones])
        S.op("dve", lambda e: e.memset(mmb[:], 1.0 / 1024.0), writes=[tmm])
        S.op("dve", lambda e: e.memset(dcols[:, DC["eps"]:DC["eps"] + 1], EPS), writes=[tdcols])
        S.op("dve", lambda e: e.memset(dcols[:, DC["ones"]:DC["ones"] + 128], 1.0), writes=[tdcols])

        def cast_group(name, pairs):
            ds = S.dsem("dc_" + name)
            t = T("scr_" + name)
            S.dma("pool", [lambda e, o=o, i=i: e.dma_start(out=o, in_=i) for (o, i) in pairs], ds, writes=[t], track=False)
            return t

        t_wvb_src = None
        casts = {}

        def do_casts(which):
            if which == "A":
                casts["wu"] = cast_group("wu", [(wu_b[k], wu[k]) for k in range(16)])
                casts["wso"] = cast_group("wso", [(wso_b[k], wso[k]) for k in range(8)])
            elif which in ("F0", "F1"):
                l = int(which[1])
                casts[f"wup{l}"] = cast_group(f"wup{l}", [(wup_b[l, j], wup[l, j]) for j in range(NJ)])
                casts[f"wdn{l}"] = cast_group(f"wdn{l}", [(wdn_b[l, k], wdn[l, k]) for k in range(8)])
                casts[f"wgp{l}"] = cast_group(f"wgp{l}", [(wgp_b[l, k], wgp[l, k]) for k in range(8)])
                casts[f"pT{l}"] = cast_group(f"pT{l}", [(pT_b[l, kk], pT[l, kk]) for kk in range(2)])
            elif which == "B":
                casts["wqk"] = cast_group("wqk", [(wqk_b[k], wqk[k]) for k in range(16)])
                casts["wvn"] = cast_group("wvn", [(wvn_b[k], wvn[k]) for k in range(2)])
                casts["won"] = cast_group("won", [(won_b[k], won[k]) for k in range(8)])

        def layer_norm_gen(pre, n, tpre, prebf, sq, tprebf, tsq, lt, affine, banks=(6, 7), norm_eng=("dve", "dve")):
            mean_sb, var, rstd, nmr, tl = lt
            S.op("act", lambda e: e.activation(out=prebf[:, :, 0:n], in_=pre[:, :, 0:n], func=AF.Copy),
                 reads=tpre, writes=tprebf)
            yield
            S.op("act", lambda e: e.activation(out=sq[:, :, 0:n], in_=pre[:, :, 0:n], func=AF.Square),
                 reads=tpre, writes=tsq)
            yield
            bm, bq = bank(banks[0]), bank(banks[1])
            for dc in range(8):
                S.op("pe", lambda e, dc=dc: e.matmul(bm[:, 0:n], lhsT=mmb[:], rhs=prebf[:, dc, 0:n],
                                                     start=(dc == 0), stop=(dc == 7)),
                     reads=tprebf + [tmm], writes=[tbank[banks[0]]])
            yield
            for dc in range(8):
                S.op("pe", lambda e, dc=dc: e.matmul(bq[:, 0:n], lhsT=mmb[:], rhs=sq[:, dc, 0:n],
                                                     start=(dc == 0), stop=(dc == 7)),
                     reads=tsq + [tmm], writes=[tbank[banks[1]]])
            yield
            S.op("act", lambda e: e.activation(out=mean_sb[:, 0:n], in_=bm[:, 0:n], func=AF.Copy),
                 reads=[tbank[banks[0]]], writes=[tl[0]])
            S.op("dve", lambda e: e.tensor_tensor(out=var[:, 0:n], in0=mean_sb[:, 0:n], in1=mean_sb[:, 0:n], op=ALU.mult),
                 reads=[tl[0]], writes=[tl[1]])
            S.op("dve", lambda e: e.tensor_tensor(out=var[:, 0:n], in0=bq[:, 0:n], in1=var[:, 0:n], op=ALU.subtract),
                 reads=[tbank[banks[1]], tl[1]], writes=[tl[1]])
            yield
            S.op("act", lambda e: e.activation(out=rstd[:, 0:n], in_=var[:, 0:n], func=AF.Sqrt, bias=dcol("eps"), scale=1.0),
                 reads=[tl[1], tdcols], writes=[tl[2]])
            S.op("dve", lambda e: e.reciprocal(out=rstd[:, 0:n], in_=rstd[:, 0:n]), reads=[tl[2]], writes=[tl[2]])
            S.op("dve", lambda e: e.scalar_tensor_tensor(out=nmr[:, 0:n], in0=mean_sb[:, 0:n], scalar=-1.0, in1=rstd[:, 0:n],
                                                         op0=ALU.mult, op1=ALU.mult),
                 reads=[tl[0], tl[2]], writes=[tl[3]])
            yield
            for dc in range(8):
                S.op(norm_eng[0], lambda e, dc=dc: e.tensor_tensor(out=pre[:, dc, 0:n], in0=pre[:, dc, 0:n], in1=rstd[:, 0:n], op=ALU.mult),
                     reads=[tpre[dc], tl[2]], writes=[tpre[dc]])
                S.op(norm_eng[1], lambda e, dc=dc: e.tensor_tensor(out=pre[:, dc, 0:n], in0=pre[:, dc, 0:n], in1=nmr[:, 0:n], op=ALU.add),
                     reads=[tpre[dc], tl[3]], writes=[tpre[dc]])
                affine(dc)
                yield

        def layer_norm(*a, **k):
            for _ in layer_norm_gen(*a, **k):
                pass

        def alloc_ln_temps(st, w):
            tiles = [sb(f"ln_{k}", [128, w], F32, st) for k in range(4)]
            return tiles + [[T(f"ln{k}") for k in range(4)]]

        def phase_A():
            do_casts("A")
            with ExitStack() as st:
                wvb = sb("wvb", [128, 8, 2048], BF16, st)
                wsb = sb("wsb", [128, 8, 128], BF16, st)
                Rt = sb("Rt", [128, 16, 128], F32, st)
                rowA = sb("rowA", [1, 3072], BF16, st)
                u = sb("u", [128, 16, 512], BF16, st)
                v32 = [sb(f"v32_{k}", [128, 2048], F32, st) for k in range(2)]
                vn = [sb(f"vn{k}", [128, 2048], BF16, st) for k in range(2)]
                tmpy = [sb(f"tmpy{k}", [128, 512], F32, st) for k in range(2)]
                pre = sb("preA", [128, 8, 512], F32, st)
                stt = [sb(f"stt{k}", [128, 32], F32, st) for k in range(2)]
                lt = alloc_ln_temps(st, 512)
                twvb, twsb, tRt, trowA = T(), T(), T(), T()
                tpre = [T() for _ in range(8)]
                tvn, tstt = [T(), T()], [T(), T()]
                tu = [T(f"u{k}") for k in range(16)]
                tv32 = [T(), T()]
                ttmpy = [T(), T()]
                ring_u = Ring(S, nc, st, "ru", 4, [8, 128], [wu_b[fc] for _ in range(NB) for fc in range(16)], [casts["wu"]])
                ring_wo = Ring(S, nc, st, "rwo", 2, [16, 128], [wso_b[dc] for _ in range(NB) for dc in range(8)], [casts["wso"]])
                dA = S.dsem("dA")
                S.dma("pool", [lambda e, kc=kc: e.dma_start(out=wvb[:, kc, :], in_=wv[:, kc, :]) for kc in range(8)], dA, writes=[twvb])
                dA2 = S.dsem("dA2")
                S.dma("pool", [lambda e: e.dma_start(out=wsb[:], in_=wst)], dA2, writes=[twsb])
                dA3 = S.dsem("dA3")
                S.dma("pool", [lambda e: e.dma_start(out=rowA[0:1, 0:2048], in_=rows[0:1, R_BINV:R_BINV + 2048]),
                               lambda e: e.dma_start(out=rowA[0:1, 2048:3072], in_=rows[0:1, R_BOUT:R_BOUT + 1024])],
                      dA3, writes=[trowA])
                for w_ in ("F0", "B", "F1"):
                    do_casts(w_)
                dA4 = S.dsem("dA4")
                S.dma("sp", [lambda e, a=a: e.dma_start(out=pre[:, a, :],
                                                        in_=rows[0:1, R_LNB + a * 512:R_LNB + a * 512 + 512].to_broadcast([128, 512]))
                             for a in range(4)]
                      + [lambda e, a=a: e.dma_start(out=pre[0:1, 4 + a, :], in_=rows[0:1, R_BS + a * 512:R_BS + a * 512 + 512])
                         for a in range(2)]
                      + [lambda e, a=a: e.dma_start(out=pre[:, 6 + a, :].rearrange("p (g t) -> p g t", g=4), in_=wst[:, 4 * a:4 * a + 4, :])
                         for a in range(2)],
                      dA4, writes=tpre)
                for cc in range(16):
                    g = cc // 2
                    pb = bank(cc % 2)
                    S.op("pe", lambda e, cc=cc, g=g, pb=pb: e.matmul(
                        pb[:, 0:128], lhsT=pre[:, cc // 4, (cc % 4) * 128:(cc % 4) * 128 + 128],
                        rhs=pre[:, 6 + g // 4, (g % 4) * 128:(g % 4) * 128 + 128], start=True, stop=False),
                        reads=tpre, writes=[tbank[cc % 2]])
                    S.op("pe", lambda e, cc=cc, g=g, pb=pb: e.matmul(
                        pb[:, 0:128], lhsT=dcols[0:1, DC["ones"]:DC["ones"] + 128],
                        rhs=pre[0:1, 4 + g // 4, (g % 4) * 128:(g % 4) * 128 + 128], start=False, stop=True),
                        reads=tpre + [tdcols], writes=[tbank[cc % 2]])
                    S.op("act", lambda e, cc=cc, pb=pb: e.activation(out=Rt[:, cc, :], in_=pb[:, 0:128], func=AF.Copy),
                         reads=[tbank[cc % 2]], writes=[tRt])

                prebf_v = v32[0][:].bitcast(BF16).rearrange("p (a b) -> p a b", a=8)
                sq_v = v32[1][:].bitcast(BF16).rearrange("p (a b) -> p a b", a=8)
                pending = None
                for i in range(NB):
                    t0 = 512 * i
                    txi = tx(t0, t0 + 512)
                    for fc in range(16):
                        slot = ring_u.get(i * 16 + fc)
                        pb = bank(fc % 2)
                        for kc in range(8):
                            S.op("pe", lambda e, pb=pb, slot=slot, kc=kc: e.matmul(
                                pb, lhsT=ring_u.tile[:, slot, kc, :], rhs=xs[:, kc, t0:t0 + 512], start=(kc == 0), stop=(kc == 7)),
                                reads=[ring_u.Ts[slot]] + txi, writes=[tbank[fc % 2]])
                        S.op("act", lambda e, pb=pb, fc=fc: e.activation(out=u[:, fc, :], in_=pb, func=AF.Gelu,
                                                                          bias=col("bu_in", fc), scale=1.0),
                             reads=[tbank[fc % 2], tcols], writes=[tu[fc]])
                        if pending is not None:
                            for _ in range(2):
                                next(pending, None)
                    if pending is not None:
                        for _ in pending:
                            pass
                        pending = None

                    def vmm(c, fbs):
                        tc0 = t0 + 128 * c
                        vb, tvb = v32[c % 2], tv32[c % 2]
                        sttc = stt[c % 2]
                        for fb in fbs:
                            pb = bank(2 + fb % 2)
                            for kc in range(8):
                                S.op("pe", lambda e, pb=pb, kc=kc, fb=fb: e.matmul(
                                    pb, lhsT=xs[:, kc, tc0:tc0 + 128], rhs=wvb[:, kc, fb * 512:fb * 512 + 512],
                                    start=(kc == 0), stop=False),
                                    reads=[twvb] + txi, writes=[tbank[2 + fb % 2]])
                            S.op("pe", lambda e, pb=pb, fb=fb: e.matmul(
                                pb, lhsT=onesb[0:1, 0:128], rhs=rowA[0:1, fb * 512:fb * 512 + 512], start=False, stop=True),
                                reads=[tones, trowA], writes=[tbank[2 + fb % 2]])
                            S.op("act", lambda e, pb=pb, fb=fb, vb=vb: e.activation(out=vb[:, fb * 512:fb * 512 + 512], in_=pb, func=AF.Gelu),
                                 reads=[tbank[2 + fb % 2]], writes=[tvb])
                            S.op("dve", lambda e, fb=fb, vb=vb, sttc=sttc: e.bn_stats(out=sttc[:, fb * 6:fb * 6 + 6], in_=vb[:, fb * 512:fb * 512 + 512]),
                                 reads=[tvb], writes=[tstt[c % 2]])

                    def stats_norm(c):
                        vb, tvb = v32[c % 2], tv32[c % 2]
                        sttc, ts = stt[c % 2], tstt[c % 2]
                        S.op("dve", lambda e: e.bn_aggr(out=sttc[:, 24:26], in_=sttc[:, 0:24]), reads=[ts], writes=[ts])
                        S.op("act", lambda e: e.activation(out=sttc[:, 26:27], in_=sttc[:, 25:26], func=AF.Sqrt, bias=dcol("eps"), scale=1.0),
                             reads=[ts, tdcols], writes=[ts])
                        S.op("dve", lambda e: e.reciprocal(out=sttc[:, 26:27], in_=sttc[:, 26:27]), reads=[ts], writes=[ts])
                        S.op("dve", lambda e: e.scalar_tensor_tensor(out=sttc[:, 27:28], in0=sttc[:, 24:25], scalar=-1.0, in1=sttc[:, 26:27],
                                                                     op0=ALU.mult, op1=ALU.mult), reads=[ts], writes=[ts])
                        S.op("act", lambda e: e.activation(out=vn[c % 2][:], in_=vb[:], func=AF.Identity, bias=sttc[:, 27:28], scale=sttc[:, 26:27]),
                             reads=[tvb, ts], writes=[tvn[c % 2]])

                    def spatial(c):
                        vnc, tvnc = vn[c % 2], tvn[c % 2]
                        for q4 in range(4):
                            pb = bank(4 + q4)
                            for k in range(4):
                                cc = q4 * 4 + k
                                g = cc // 2
                                S.op("pe", lambda e, pb=pb, k=k, cc=cc, g=g: e.matmul(
                                    pb[:, k * 128:k * 128 + 128], lhsT=vnc[:, cc * 128:cc * 128 + 128], rhs=wsb[:, g, :], start=True, stop=True),
                                    reads=[tvnc, twsb], writes=[tbank[4 + q4]])
                        for q4 in range(4):
                            pb = bank(4 + q4)
                            ty, tty = tmpy[q4 % 2], ttmpy[q4 % 2]
                            for k in range(4):
                                cc = q4 * 4 + k
                                S.op("act", lambda e, pb=pb, k=k, cc=cc, ty=ty: e.activation(
                                    out=ty[:, k * 128:k * 128 + 128], in_=pb[:, k * 128:k * 128 + 128], func=AF.Identity, scale=col("lng", cc)),
                                    reads=[tbank[4 + q4], tcols], writes=[tty])
                            S.op("dve", lambda e, ty=ty, q4=q4: e.tensor_tensor(
                                out=ty[:], in0=ty[:], in1=Rt[:, q4 * 4:q4 * 4 + 4, :].rearrange("p a b -> p (a b)"), op=ALU.add),
                                reads=[tRt], writes=[tty])
                            S.op("dve", lambda e, ty=ty, q4=q4: e.tensor_tensor(
                                out=u[:, q4 * 4:q4 * 4 + 4, c * 128:c * 128 + 128], in0=ty[:].rearrange("p (a b) -> p a b", a=4),
                                in1=u[:, q4 * 4:q4 * 4 + 4, c * 128:c * 128 + 128], op=ALU.mult),
                                reads=[tty] + tu[q4 * 4:q4 * 4 + 4], writes=tu[q4 * 4:q4 * 4 + 4])

                    vmm(0, range(4))
                    for c in range(4):
                        if c < 3:
                            vmm(c + 1, (0, 1))
                        stats_norm(c)
                        if c < 3:
                            vmm(c + 1, (2, 3))
                        spatial(c)
                    for dc in range(8):
                        slot = ring_wo.get(i * 8 + dc)
                        pb = bank(6 + dc % 2)
                        for fc in range(16):
                            S.op("pe", lambda e, pb=pb, slot=slot, fc=fc: e.matmul(
                                pb, lhsT=ring_wo.tile[:, slot, fc, :], rhs=u[:, fc, :], start=(fc == 0), stop=False),
                                reads=[ring_wo.Ts[slot], tu[fc]], writes=[tbank[6 + dc % 2]])
                        S.op("pe", lambda e, pb=pb, dc=dc: e.matmul(
                            pb, lhsT=rowA[0:1, 2048 + dc * 128:2048 + dc * 128 + 128], rhs=onesb[0:1, 0:512], start=False, stop=True),
                            reads=[trowA, tones], writes=[tbank[6 + dc % 2]])
                        S.op("dve", lambda e, pb=pb, dc=dc: e.scalar_tensor_tensor(
                            out=pre[:, dc, :], in0=xs[:, dc, t0:t0 + 512], scalar=ALPHA, in1=pb, op0=ALU.mult, op1=ALU.add),
                            reads=[tbank[6 + dc % 2]] + txi, writes=[tpre[dc]])

                    def affine(dc, t0=t0, txi=txi):
                        S.op("act", lambda e, dc=dc: e.activation(out=xs[:, dc, t0:t0 + 512], in_=pre[:, dc, :], func=AF.Identity,
                                                                  bias=col("ln1b0", dc), scale=col("ln1g0", dc)),
                             reads=[tpre[dc], tcols], writes=txi)

                    pending = layer_norm_gen(pre, 512, tpre, prebf_v, sq_v, [tv32[0]], [tv32[1]], lt, affine)
                for _ in pending:
                    pass
                S.barrier()

        t_rpscr = T("rpbt_scr")

        def bias_exp_gen(st):
            bt32 = sb("bt32", [128, 11, 128], F32, st)
            bte = sb("bte", [128, 11, 128], BF16, st)
            tbt32, tbte = T(), T()
            dbt = S.dsem("dbt")
            dscr = S.dsem("dscr")
            for h in range(16):
                for (lo, n_) in ((0, 11), (11, 10)):
                    S.dma("sp", [lambda e, h=h, lo=lo, n_=n_: e.dma_start(out=bt32[:, 0:n_, :], in_=rpbt[h, :, lo:lo + n_, :])], dbt, writes=[tbt32])
                    yield
                    S.op("act", lambda e, n_=n_: e.activation(out=bte[:, 0:n_, :], in_=bt32[:, 0:n_, :], func=AF.Exp), reads=[tbt32], writes=[tbte])
                    S.dma("sp", [lambda e, h=h, lo=lo, n_=n_: e.dma_start(out=rpbt_b[h, :, lo:lo + n_, :], in_=bte[:, 0:n_, :])], dscr,
                          reads=[tbte], writes=[t_rpscr])
                    yield

        def phase_F(l, final):
            with ExitStack() as st:
                NCB = 4
                act = sb("act", [128, NJ, 514], BF16, st)
                cbig = sb("cbig", [128, NCB, 2, 514], F32, st)
                cag = [cbig[:, k] for k in range(NCB)]
                ga = [sb(f"ga{k}", [128, 514], F32, st) for k in range(3)]
                carry = [sb(f"carry{k}", [128, 2, NJ], F32, st) for k in range(2)]
                hl = [sb(f"hl{k}", [128, 2, NJ], F32, st) for k in range(2)]
                rowF = sb("rowF", [1, 1024], BF16, st)
                pres = [sb(f"preF{k}", [128, 8, 512], F32, st) for k in range(2)]
                prebf = cbig[:, 0:2].rearrange("p a b c -> p (a b c)").bitcast(BF16)[:, 0:4096].rearrange("p (a b) -> p a b", a=8)
                sq = cbig[:, 2:4].rearrange("p a b c -> p (a b c)").bitcast(BF16)[:, 0:4096].rearrange("p (a b) -> p a b", a=8)
                sg = [sb(f"sg{k}", [128, 512], F32, st) for k in range(2)]
                lt = alloc_ln_temps(st, 512)
                if os.environ.get("KVERBOSE"):
                    print("phase F sbuf remaining before rings", nc.sbuf_bytes_remaining)
                tact = [T(f"act{j}") for j in range(NJ)]
                tcag, tga = [T() for _ in range(NCB)], [T() for _ in range(3)]
                tcarry, thl = [T(), T()], [T(), T()]
                trowF = T()
                tpres = [[T() for _ in range(8)] for _ in range(2)]
                tsg = [T(), T()]
                segs_all = []
                for i in range(NB):
                    t0 = 512 * i
                    if i == 0:
                        segs_all.append([(1, 511, 0)])
                    elif i < NB - 1:
                        segs_all.append([(0, 512, t0 - 1)])
                    else:
                        segs_all.append([(0, 512, t0 - 1), (512, 1, SEQ - 1)])
                nseg = sum(len(s) for s in segs_all)
                flat = [s for ss in segs_all for s in ss]
                extra_gen = bias_exp_gen(st) if l == 0 else None
                ring_up = Ring(S, nc, st, "rup", 4 if l == 0 else 6, [8, 256], [wup_b[l, j] for _ in range(NB) for j in range(NJ)], [casts[f"wup{l}"]])
                ring_dn = Ring(S, nc, st, "rdn", 2, [NJ, 128], [wdn_b[l, dc] for _ in range(nseg) for dc in range(8)], [casts[f"wdn{l}"]])
                ring_gp = Ring(S, nc, st, "rgp", 2, [10, 128], [wgp_b[l, dc] for _ in range(nseg) for dc in range(8)], [casts[f"wgp{l}"]])
                ring_p = Ring(S, nc, st, "rp", 2, [2, 512],
                              [(pT_b[l, :, :, tl:tl + n].rearrange("k p n -> p k n"), (lambda d, n=n: d[:, :, 0:n])) for (_, n, tl) in flat],
                              [casts[f"pT{l}"]])
                dF = S.dsem("dF")
                rb = R_BDN0 if l == 0 else R_BDN1
                S.dma("pool", [lambda e: e.dma_start(out=rowF[0:1, :], in_=rows[0:1, rb:rb + 1024])], dF, writes=[trowF])
                for s_ in ("a", "g"):
                    w0, w1, w2 = (colsb[:, COLS[f"w{k}{s_}{l}"]:COLS[f"w{k}{s_}{l}"] + NJ] for k in range(3))
                    bu = colsb[:, COLS[f"bu{s_}{l}"]:COLS[f"bu{s_}{l}"] + NJ]
                    cb = colsb[:, COLS[f"cb{s_}{l}"]:COLS[f"cb{s_}{l}"] + NJ]
                    tmp = dcols[:, DC[f"tmp{l}"]:DC[f"tmp{l}"] + NJ]
                    cK = dcols[:, DC[f"cK{s_}{l}"]:DC[f"cK{s_}{l}"] + NJ]
                    ne0 = dcols[:, DC[f"ne0{s_}{l}"]:DC[f"ne0{s_}{l}"] + NJ]
                    ne2 = dcols[:, DC[f"ne2{s_}{l}"]:DC[f"ne2{s_}{l}"] + NJ]
                    S.op("dve", lambda e, tmp=tmp, w0=w0, w1=w1: e.tensor_tensor(out=tmp, in0=w0, in1=w1, op=ALU.add), reads=[tcols, tdcols], writes=[tdcols])
                    S.op("dve", lambda e, tmp=tmp, w2=w2: e.tensor_tensor(out=tmp, in0=tmp, in1=w2, op=ALU.add), reads=[tcols, tdcols], writes=[tdcols])
                    S.op("dve", lambda e, tmp=tmp, bu=bu: e.tensor_tensor(out=tmp, in0=tmp, in1=bu, op=ALU.mult), reads=[tcols, tdcols], writes=[tdcols])
                    S.op("dve", lambda e, tmp=tmp, cb=cb, cK=cK: e.tensor_tensor(out=cK, in0=tmp, in1=cb, op=ALU.add), reads=[tcols, tdcols], writes=[tdcols])
                    S.op("dve", lambda e, ne0=ne0, w0=w0, bu=bu: e.scalar_tensor_tensor(out=ne0, in0=w0, scalar=-1.0, in1=bu, op0=ALU.mult, op1=ALU.mult),
                         reads=[tcols, tdcols], writes=[tdcols])
                    S.op("dve", lambda e, ne2=ne2, w2=w2, bu=bu: e.scalar_tensor_tensor(out=ne2, in0=w2, scalar=-1.0, in1=bu, op0=ALU.mult, op1=ALU.mult),
                         reads=[tcols, tdcols], writes=[tdcols])
                bg = colsb[:, COLS[f"bgate{l}"]:COLS[f"bgate{l}"] + 8]
                hbg = dcols[:, DC[f"hbg{l}"]:DC[f"hbg{l}"] + 8]
                S.op("dve", lambda e: e.tensor_scalar(out=hbg, in0=bg, scalar1=0.5, scalar2=None, op0=ALU.mult), reads=[tcols, tdcols], writes=[tdcols])
                for k in range(NCB):
                    S.op("dve", lambda e, k=k: e.memset(cag[k], 0.0), writes=[tcag[k]])

                def up_conv(i, bg_gen):
                    t0 = 512 * i
                    txi = tx(t0, t0 + 512)
                    last = (i == NB - 1)
                    hl_r, hl_w = hl[(i + 1) % 2], hl[i % 2]
                    thl_r, thl_w = thl[(i + 1) % 2], thl[i % 2]
                    cy_r, cy_w = carry[(i + 1) % 2], carry[i % 2]
                    tcy_r, tcy_w = tcarry[(i + 1) % 2], tcarry[i % 2]
                    for j in range(NJ):
                        slot = ring_up.get(i * NJ + j)
                        ba, bgk = j % 3, 3 + j % 3
                        pa, pg = bank(ba), bank(bgk)
                        for (pb, off, bk) in ((pa, 0, ba), (pg, 128, bgk)):
                            for kc in range(8):
                                S.op("pe", lambda e, pb=pb, slot=slot, kc=kc, off=off: e.matmul(
                                    pb, lhsT=ring_up.tile[:, slot, kc, off:off + 128], rhs=xs[:, kc, t0:t0 + 512], start=(kc == 0), stop=(kc == 7)),
                                    reads=[ring_up.Ts[slot]] + txi, writes=[tbank[bk]])
                        cb_ = j % NCB
                        c2, tc = cag[cb_], tcag[cb_]
                        for (pb, bk, s_, si) in ((pa, ba, "a", 0), (pg, bgk, "g", 1)):
                            w0, w1 = col(f"w0{s_}{l}", j), col(f"w1{s_}{l}", j)
                            cK = dcol(f"cK{s_}{l}", j)
                            S.op("act", lambda e, c2=c2, si=si, pb=pb, w1=w1, cK=cK: e.activation(out=c2[:, si, 1:513], in_=pb, func=AF.Identity, bias=cK, scale=w1),
                                 reads=[tbank[bk], tcols, tdcols], writes=[tc])
                            if not last:
                                S.op("act", lambda e, pb=pb, si=si, j=j, w0=w0: e.activation(out=hl_w[:, si, j:j + 1], in_=pb[:, 511:512], func=AF.Identity, scale=w0),
                                     reads=[tbank[bk], tcols], writes=[thl_w])
                        if i > 0:
                            S.op("dve", lambda e, c2=c2, j=j: e.tensor_copy(out=c2[:, :, 0], in_=cy_r[:, :, j]), reads=[tcy_r], writes=[tc])
                        for (pb, bk, s_, si) in ((pa, ba, "a", 0), (pg, bgk, "g", 1)):
                            w0 = col(f"w0{s_}{l}", j)
                            S.op("dve", lambda e, c2=c2, si=si, pb=pb, w0=w0: e.scalar_tensor_tensor(out=c2[:, si, 2:513], in0=pb[:, 0:511], scalar=w0, in1=c2[:, si, 2:513],
                                                                                                   op0=ALU.mult, op1=ALU.add),
                                 reads=[tbank[bk], tcols], writes=[tc])
                        if i > 0:
                            S.op("dve", lambda e, c2=c2, j=j: e.tensor_tensor(out=c2[:, :, 1], in0=c2[:, :, 1], in1=hl_r[:, :, j], op=ALU.add),
                                 reads=[thl_r], writes=[tc])
                        else:
                            o0 = DC[f"ne0a{l}"] + j
                            S.op("dve", lambda e, c2=c2, o0=o0: e.tensor_tensor(out=c2[:, :, 1], in0=c2[:, :, 1], in1=dcols[:, o0:o0 + NJ + 1:NJ], op=ALU.add),
                                 reads=[tdcols], writes=[tc])
                        if not last:
                            S.op("dve", lambda e, c2=c2, j=j: e.tensor_copy(out=cy_w[:, :, j], in_=c2[:, :, 512]), reads=[tc], writes=[tcy_w])
                        else:
                            o2 = DC[f"ne2a{l}"] + j
                            S.op("dve", lambda e, c2=c2, o2=o2: e.tensor_tensor(out=c2[:, :, 512], in0=c2[:, :, 512], in1=dcols[:, o2:o2 + NJ + 1:NJ], op=ALU.add),
                                 reads=[tdcols], writes=[tc])
                        for (pb, bk, s_, si) in ((pa, ba, "a", 0), (pg, bgk, "g", 1)):
                            w2 = col(f"w2{s_}{l}", j)
                            S.op("dve", lambda e, c2=c2, si=si, pb=pb, w2=w2: e.scalar_tensor_tensor(out=c2[:, si, 0:512], in0=pb[:, 0:512], scalar=w2, in1=c2[:, si, 0:512],
                                                                                                   op0=ALU.mult, op1=ALU.add),
                                 reads=[tbank[bk], tcols], writes=[tc])

                        def finish(jj):
                            cbp = jj % NCB
                            cp, tcp = cag[cbp], tcag[cbp]
                            gk, tgk = ga[jj % 3], tga[jj % 3]
                            S.op("act", lambda e, gk=gk, cp=cp: e.activation(out=gk[:, 0:513], in_=cp[:, 0, 0:513], func=AF.Gelu_apprx_tanh),
                                 reads=[tcp], writes=[tgk])
                            S.op("dve", lambda e, gk=gk, jj=jj, cp=cp: e.tensor_tensor(out=act[:, jj, 0:513], in0=gk[:, 0:513], in1=cp[:, 1, 0:513], op=ALU.mult),
                                 reads=[tgk, tcp], writes=[tact[jj]])

                        if j > 0:
                            finish(j - 1)
                        if j == NJ - 1:
                            finish(j)
                        if bg_gen is not None:
                            next(bg_gen, None)
                        if extra_gen is not None:
                            next(extra_gen, None)
                    if bg_gen is not None:
                        for _ in bg_gen:
                            pass

                def down(seg_idx, c0, n, tl, bg_gen):
                    txs_seg = tx(tl, tl + n)
                    pre, tpre = pres[seg_idx % 2], tpres[seg_idx % 2]
                    bg_sched = [0, 2, 2, 3, 3, 2, 0, 0]
                    if bg_gen is not None:
                        for _ in range(2):
                            next(bg_gen, None)
                    for dc in range(8):
                        slot = ring_dn.get(seg_idx * 8 + dc)
                        bk = 6 + dc % 2
                        pb = bank(bk)
                        for fc in range(NJ):
                            S.op("pe", lambda e, pb=pb, slot=slot, fc=fc: e.matmul(
                                pb[:, 0:n], lhsT=ring_dn.tile[:, slot, fc, :], rhs=act[:, fc, c0:c0 + n], start=(fc == 0), stop=False),
                                reads=[ring_dn.Ts[slot], tact[fc]], writes=[tbank[bk]])
                        S.op("pe", lambda e, pb=pb, dc=dc: e.matmul(
                            pb[:, 0:n], lhsT=rowF[0:1, dc * 128:dc * 128 + 128], rhs=onesb[0:1, 0:n], start=False, stop=True),
                            reads=[trowF, tones], writes=[tbank[bk]])
                        S.op("dve", lambda e, pb=pb, dc=dc: e.scalar_tensor_tensor(
                            out=pre[:, dc, 0:n], in0=xs[:, dc, tl:tl + n], scalar=ALPHA, in1=pb[:, 0:n], op0=ALU.mult, op1=ALU.add),
                            reads=[tbank[bk]] + txs_seg, writes=[tpre[dc]])
                        if bg_gen is not None:
                            for _ in range(bg_sched[dc]):
                                next(bg_gen, None)

                def tail_gen(seg_idx, c0, n, tl):
                    txs_seg = tx(tl, tl + n)
                    pre, tpre = pres[seg_idx % 2], tpres[seg_idx % 2]
                    pslot = ring_p.get(seg_idx)

                    def affine(dc):
                        S.op("act", lambda e, dc=dc: e.activation(out=pre[:, dc, 0:n], in_=pre[:, dc, 0:n], func=AF.Identity,
                                                                  bias=col(f"ln2b{l}", dc), scale=col(f"ln2g{l}", dc)),
                             reads=[tpre[dc], tcols], writes=[tpre[dc]])
                        S.op("act", lambda e, dc=dc: e.activation(out=xs[:, dc, tl:tl + n], in_=pre[:, dc, 0:n], func=AF.Copy),
                             reads=[tpre[dc]], writes=txs_seg)

                    yield from layer_norm_gen(pre, n, tpre, prebf, sq, tcag[0:2], tcag[2:4], lt, affine, banks=(0, 1), norm_eng=("dve", "dve"))
                    for dc in range(8):
                        slot = ring_gp.get(seg_idx * 8 + dc)
                        bgt, bpj = 6, 7
                        pgt, ppj = bank(bgt), bank(bpj)
                        for kc in range(8):
                            S.op("pe", lambda e, pgt=pgt, slot=slot, kc=kc: e.matmul(
                                pgt[:, 0:n], lhsT=ring_gp.tile[:, slot, kc, :], rhs=xs[:, kc, tl:tl + n], start=(kc == 0), stop=(kc == 7)),
                                reads=[ring_gp.Ts[slot]] + txs_seg, writes=[tbank[bgt]])
                        for kk in range(2):
                            S.op("pe", lambda e, ppj=ppj, slot=slot, kk=kk: e.matmul(
                                ppj[:, 0:n], lhsT=ring_gp.tile[:, slot, 8 + kk, :], rhs=ring_p.tile[:, pslot, kk, 0:n], start=(kk == 0), stop=(kk == 1)),
                                reads=[ring_gp.Ts[slot], ring_p.Ts[pslot]], writes=[tbank[bpj]])
                        sgk = sg[dc % 2]
                        S.op("act", lambda e, sgk=sgk, pgt=pgt, dc=dc: e.activation(out=sgk[:, 0:n], in_=pgt[:, 0:n], func=AF.Tanh,
                                                                                    bias=dcol(f"hbg{l}", dc), scale=0.5),
                             reads=[tbank[bgt], tdcols], writes=[tsg[dc % 2]])
                        S.op("dve", lambda e, sgk=sgk, ppj=ppj: e.scalar_tensor_tensor(out=sgk[:, 0:n], in0=sgk[:, 0:n], scalar=1.0, in1=ppj[:, 0:n],
                                                                                      op0=ALU.add, op1=ALU.mult),
                             reads=[tbank[bpj]], writes=[tsg[dc % 2]])
                        S.op("dve", lambda e, sgk=sgk, dc=dc: e.scalar_tensor_tensor(out=pre[:, dc, 0:n], in0=sgk[:, 0:n], scalar=0.5, in1=pre[:, dc, 0:n],
                                                                                    op0=ALU.mult, op1=ALU.add),
                             reads=[tsg[dc % 2]], writes=[tpre[dc]])
                        yield
                    if final:
                        dO = dOut[seg_idx % 2]
                        S.dma("sp", [lambda e: e.dma_start(out=outT[:, :, tl:tl + n].rearrange("d p n -> p d n"), in_=pre[:, :, 0:n],
                                                           allow_slow_non_contiguous=True)],
                              dO, reads=tpre)
                    else:
                        for dc in range(8):
                            S.op("act", lambda e, dc=dc: e.activation(out=xs[:, dc, tl:tl + n], in_=pre[:, dc, 0:n], func=AF.Copy),
                                 reads=[tpre[dc]], writes=txs_seg)
                            if dc % 2 == 1:
                                yield

                genA = None
                genB = None
                seg_idx = 0
                for i in range(NB):
                    up_conv(i, genA)
                    genA = None
                    for (c0, n, tl) in segs_all[i]:
                        if genA is not None:
                            for _ in genA:
                                pass
                        down(seg_idx, c0, n, tl, genB)
                        genA = genB
                        genB = tail_gen(seg_idx, c0, n, tl)
                        seg_idx += 1
                for g_ in (genA, genB):
                    if g_ is not None:
                        for _ in g_:
                            pass
                if extra_gen is not None:
                    for _ in extra_gen:
                        pass
                S.barrier()


        def window(m):
            if m == 0:
                return [0, 1, 2, 3], [0, 1, 2, 3]
            if m == 1:
                return [0, 1, 2, 3], [4, 5, 6, 7]
            if m == 30:
                return [28, 29, 30, 31], [0, 1, 2, 3]
            if m == 31:
                return [28, 29, 30, 31], [4, 5, 6, 7]
            return [m - 2, m - 1, m, m + 1, m + 2], [0, 1, 2, 3, 4]

        def phase_B():
            NTB = 16
            with ExitStack() as st:
                Kr = sb("Kr", [128, 8, 1024], BF16, st)
                Vr = sb("Vr", [128, 8, 16, 65], BF16, st)
                Qt = sb("Qt", [128, 8, 256], BF16, st)
                E = [sb(f"E{k}", [128, 640], BF16, st) for k in range(3)]
                otok = sb("otok", [128, 2, 1024], BF16, st)
                oT = sb("oT", [128, 8, 256], BF16, st)
                rowB = sb("rowB", [1, 2048], BF16, st)
                pre = sb("preB", [128, 8, 256], F32, st)
                prebf = sb("prebfB", [128, 8, 256], BF16, st)
                sq = sb("sqB", [128, 8, 256], BF16, st)
                rc = [sb(f"rc{k}", [128, 1], F32, st) for k in range(2)]
                lt = alloc_ln_temps(st, 256)
                tK = [T(f"K{k}") for k in range(4)]
                tV = [T(f"V{k}") for k in range(8)]
                tQ, tE, totok, toT, trowB, tprebf, tsq = T(), [T(), T(), T()], [T(), T()], T(), T(), T(), T()
                tpre = [T() for _ in range(8)]
                trc = [T(), T()]
                qk_srcs, v_srcs = [], []
                order = [("kv", 0)]
                for ti in range(NTB):
                    if ti + 1 < NTB:
                        order.append(("kv", ti + 1))
                    order.append(("q", ti))
                for (kind, _) in order:
                    if kind == "kv":
                        qk_srcs += [wqk_b[8 + fc] for fc in range(8)]
                        v_srcs += [wvn_b[vb] for vb in range(2)]
                    else:
                        qk_srcs += [wqk_b[fc] for fc in range(8)]
                ring_qk = Ring(S, nc, st, "rqk", 4, [8, 128], qk_srcs, [casts["wqk"]])
                ring_v = Ring(S, nc, st, "rv", 2, [8, 512], v_srcs, [casts["wvn"]])
                ring_won = Ring(S, nc, st, "rwon", 3, [8, 128], [won_b[dc] for _ in range(NTB) for dc in range(8)], [casts["won"]])
                bias_srcs = []
                for ti in (0, NTB - 1):
                    lo, nt = (5, 8) if ti == 0 else (13, 8)
                    for h in range(16):
                        bias_srcs.append((rpbt_b[h, :, lo:lo + nt, :], (lambda d, nt=nt: d[:, 0:nt, :])))
                bias_res = sb("bias_res", [128, 16, 5, 128], BF16, st)
                tbres = T("bias_res")
                dbres = S.dsem("dbres")
                S.dma("sp", [lambda e, h=h: e.dma_start(out=bias_res[:, h], in_=rpbt_b[h, :, 0:5, :]) for h in range(16)], dbres,
                      reads=[t_rpscr], writes=[tbres])
                ring_b = Ring(S, nc, st, "rb", 3, [8, 128], bias_srcs, [t_rpscr])
                dB = S.dsem("dB")
                S.dma("pool", [lambda e: e.dma_start(out=rowB[0:1, 0:1024], in_=rows[0:1, R_BV:R_BV + 1024]),
                               lambda e: e.dma_start(out=rowB[0:1, 1024:2048], in_=rows[0:1, R_BO:R_BO + 1024])], dB, writes=[trowB])
                S.op("dve", lambda e: e.memset(Vr[:, :, :, 64:65], 1.0), writes=tV)
                cnt = {"qk": 0, "v": 0}

                bgB = [None]

                def step_bg(k=1):
                    if bgB[0] is not None:
                        for _ in range(k):
                            next(bgB[0], None)

                def kv_proj(ti):
                    tk0 = 256 * ti
                    txi = tx(tk0, tk0 + 256)
                    ks = ti % 4
                    for fc in range(8):
                        slot = ring_qk.get(cnt["qk"])
                        cnt["qk"] += 1
                        pb = bank(fc % 2)
                        for kc in range(8):
                            S.op("pe", lambda e, pb=pb, slot=slot, kc=kc: e.matmul(
                                pb[:, 0:256], lhsT=ring_qk.tile[:, slot, kc, :], rhs=xs[:, kc, tk0:tk0 + 256], start=(kc == 0), stop=(kc == 7)),
                                reads=[ring_qk.Ts[slot]] + txi, writes=[tbank[fc % 2]])
                        S.op("act", lambda e, pb=pb, fc=fc: e.activation(out=Kr[:, fc, ks * 256:ks * 256 + 256], in_=pb[:, 0:256], func=AF.Identity,
                                                                          bias=col("bk", fc), scale=1.0),
                             reads=[tbank[fc % 2], tcols], writes=[tK[ks]])
                        step_bg()
                    for vb in range(2):
                        slot = ring_v.get(cnt["v"])
                        cnt["v"] += 1
                        for blk in range(2):
                            tb0 = tk0 + 128 * blk
                            vs = (2 * ti + blk) % 8
                            bk = 2 + (vb * 2 + blk) % 2
                            pb = bank(bk)
                            for kc in range(8):
                                S.op("pe", lambda e, pb=pb, slot=slot, kc=kc, tb0=tb0: e.matmul(
                                    pb, lhsT=xs[:, kc, tb0:tb0 + 128], rhs=ring_v.tile[:, slot, kc, :], start=(kc == 0), stop=False),
                                    reads=[ring_v.Ts[slot]] + txi, writes=[tbank[bk]])
                            S.op("pe", lambda e, pb=pb, vb=vb: e.matmul(
                                pb, lhsT=onesb[0:1, 0:128], rhs=rowB[0:1, vb * 512:vb * 512 + 512], start=False, stop=True),
                                reads=[tones, trowB], writes=[tbank[bk]])
                            S.op("dve", lambda e, pb=pb, vs=vs, vb=vb: e.tensor_copy(
                                out=Vr[:, vs, vb * 8:vb * 8 + 8, 0:64], in_=pb.rearrange("p (h d) -> p h d", h=8)),
                                reads=[tbank[bk]], writes=[tV[vs]])
                            step_bg()

                def q_proj(ti, after_first=None):
                    tk0 = 256 * ti
                    txi = tx(tk0, tk0 + 256)
                    for fc in range(8):
                        slot = ring_qk.get(cnt["qk"])
                        cnt["qk"] += 1
                        pb = bank(fc % 2)
                        for kc in range(8):
                            S.op("pe", lambda e, pb=pb, slot=slot, kc=kc: e.matmul(
                                pb[:, 0:256], lhsT=ring_qk.tile[:, slot, kc, :], rhs=xs[:, kc, tk0:tk0 + 256], start=(kc == 0), stop=(kc == 7)),
                                reads=[ring_qk.Ts[slot]] + txi, writes=[tbank[fc % 2]])
                        S.op("dve", lambda e, pb=pb, fc=fc: e.tensor_scalar(out=Qt[:, fc, :], in0=pb[:, 0:256], scalar1=col("bq", fc), scalar2=0.125,
                                                                            op0=ALU.add, op1=ALU.mult),
                             reads=[tbank[fc % 2], tcols], writes=[tQ])
                        step_bg()
                        if fc == 0 and after_first is not None:
                            after_first()

                kv_proj(0)
                unit = 0
                for ti in range(NTB):
                    tk0 = 256 * ti
                    txi = tx(tk0, tk0 + 256)
                    if ti + 1 < NTB:
                        kv_proj(ti + 1)

                    def qk_exp(h, pr, bslot, u):
                        fc, hp = h // 2, (h % 2) * 64
                        m = 2 * ti + pr
                        blocks, types = window(m)
                        nb = len(blocks)
                        sbk = 1 + u % 3
                        Sb = P2[sbk]
                        tS = [tbank[2 * sbk], tbank[2 * sbk + 1]]
                        Ek, tEk = E[u % 3], tE[u % 3]
                        for jb, (b, ty) in enumerate(zip(blocks, types)):
                            S.op("pe", lambda e, Sb=Sb, jb=jb, b=b: e.matmul(
                                Sb[:, jb * 128:jb * 128 + 128], lhsT=Kr[hp:hp + 64, fc, (b % 8) * 128:(b % 8) * 128 + 128],
                                rhs=Qt[hp:hp + 64, fc, pr * 128:pr * 128 + 128], start=True, stop=True),
                                reads=[tK[(b // 2) % 4], tQ], writes=tS)
                        S.op("act", lambda e, Sb=Sb, Ek=Ek, nb=nb: e.activation(out=Ek[:, 0:nb * 128], in_=Sb[:, 0:nb * 128], func=AF.Exp),
                             reads=tS, writes=[tEk])
                        ty0 = types[0]
                        if bslot is None:
                            btile, tbt = bias_res[:, h, ty0:ty0 + nb, :], tbres
                        else:
                            btile, tbt = ring_b.tile[:, bslot, ty0:ty0 + nb, :], ring_b.Ts[bslot]
                        S.op("dve", lambda e, Ek=Ek, nb=nb, btile=btile: e.tensor_tensor(
                            out=Ek[:, 0:nb * 128], in0=Ek[:, 0:nb * 128],
                            in1=btile.rearrange("p a b -> p (a b)"), op=ALU.mult),
                            reads=[tbt], writes=[tEk])

                    def pv_norm(h, pr, u):
                        m = 2 * ti + pr
                        blocks, _ = window(m)
                        nb = len(blocks)
                        Ek, tEk = E[u % 3], tE[u % 3]
                        ob = u % 2
                        Ob = bank(ob)
                        for jb, b in enumerate(blocks):
                            S.op("pe", lambda e, Ob=Ob, Ek=Ek, jb=jb, b=b: e.matmul(
                                Ob[:, 0:65], lhsT=Ek[:, jb * 128:jb * 128 + 128], rhs=Vr[:, b % 8, h, :], start=(jb == 0), stop=(jb == nb - 1)),
                                reads=[tEk, tV[b % 8]], writes=[tbank[ob]])
                        S.op("dve", lambda e, Ob=Ob, ob=ob: e.reciprocal(out=rc[ob][:], in_=Ob[:, 64:65]), reads=[tbank[ob]], writes=[trc[ob]])
                        S.op("dve", lambda e, Ob=Ob, ob=ob: e.tensor_scalar(out=otok[:, pr, h * 64:h * 64 + 64], in0=Ob[:, 0:64], scalar1=rc[ob][:, 0:1],
                                                                            scalar2=None, op0=ALU.mult),
                             reads=[tbank[ob], trc[ob]], writes=[totok[pr]])

                    prevs = []
                    bcache = {}

                    def get_bslot(h_):
                        if h_ not in bcache:
                            bcache[h_] = ring_b.get(h_) if ti == 0 else (ring_b.get(16 + h_) if ti == NTB - 1 else None)
                        return bcache[h_]

                    NE = 3

                    def early():
                        for idx in range(NE):
                            h_, pr_ = divmod(idx, 2)
                            qk_exp(h_, pr_, get_bslot(h_), unit + idx)
                            prevs.append((h_, pr_, unit + idx))

                    q_proj(ti, early)
                    if bgB[0] is not None:
                        for _ in bgB[0]:
                            pass
                        bgB[0] = None
                    ring_won.ahead(ti * 8)
                    ring_qk.ahead(cnt["qk"])
                    ring_v.ahead(cnt["v"])
                    for idx in range(NE, 32):
                        h_, pr_ = divmod(idx, 2)
                        if len(prevs) >= 3:
                            pv_norm(*prevs.pop(0))
                        qk_exp(h_, pr_, get_bslot(h_), unit + idx)
                        prevs.append((h_, pr_, unit + idx))
                    while prevs:
                        pv_norm(*prevs.pop(0))
                    unit += 32
                    for pr in range(2):
                        tb = bank(2 + pr).bitcast(BF16)
                        for fc in range(8):
                            S.op("pe", lambda e, tb=tb, fc=fc: e.transpose(tb[:, fc * 128:fc * 128 + 128], otok[:, pr, fc * 128:fc * 128 + 128], identb[:]),
                                 reads=[totok[pr], tident], writes=[tbank[2 + pr]])
                        S.op("act", lambda e, tb=tb: e.activation(out=oT[:, :, pr * 128:pr * 128 + 128], in_=tb.rearrange("p (f q) -> p f q", f=8), func=AF.Copy),
                             reads=[tbank[2 + pr]], writes=[toT])
                    for dc in range(8):
                        slot = ring_won.get(ti * 8 + dc)
                        pb = bank(dc % 2)
                        for fc in range(8):
                            S.op("pe", lambda e, pb=pb, slot=slot, fc=fc: e.matmul(
                                pb[:, 0:256], lhsT=ring_won.tile[:, slot, fc, :], rhs=oT[:, fc, :], start=(fc == 0), stop=False),
                                reads=[ring_won.Ts[slot], toT], writes=[tbank[dc % 2]])
                        S.op("pe", lambda e, pb=pb, dc=dc: e.matmul(
                            pb[:, 0:256], lhsT=rowB[0:1, 1024 + dc * 128:1024 + dc * 128 + 128], rhs=onesb[0:1, 0:256], start=False, stop=True),
                            reads=[trowB, tones], writes=[tbank[dc % 2]])
                        S.op("dve", lambda e, pb=pb, dc=dc: e.scalar_tensor_tensor(
                            out=pre[:, dc, :], in0=xs[:, dc, tk0:tk0 + 256], scalar=ALPHA, in1=pb[:, 0:256], op0=ALU.mult, op1=ALU.add),
                            reads=[tbank[dc % 2]] + txi, writes=[tpre[dc]])

                    def affine(dc, tk0=tk0, txi=txi):
                        S.op("act", lambda e, dc=dc: e.activation(out=xs[:, dc, tk0:tk0 + 256], in_=pre[:, dc, :], func=AF.Identity,
                                                                  bias=col("ln1b1", dc), scale=col("ln1g1", dc)),
                             reads=[tpre[dc], tcols], writes=txi)

                    bgB[0] = layer_norm_gen(pre, 256, tpre, prebf, sq, [tprebf], [tsq], lt, affine)
                for _ in bgB[0]:
                    pass
                S.barrier()

        dOut = [S.dsem("dOut0"), S.dsem("dOut1")]
        phases = [("A", phase_A), ("F0", lambda: phase_F(0, stop == "F0" and False)), ("B", phase_B), ("F1", lambda: phase_F(1, True))]
        for name, fn in phases:
            fn()
            if stop == name:
                break
        if stop is not None and stop != "F1":
            dD = S.dsem("dDump")
            S.barrier()
            S.dma("pool", [lambda e, kc=kc: e.dma_start(out=outT[kc], in_=xs[:, kc, :]) for kc in range(8)], dD, reads=txs)
        S.barrier()
        out_plan = S.get_plan()
        if os.environ.get("KVERBOSE"):
            print("ops", S.cnt, "incs", {e: len(v) for e, v in out_plan.items()}, "nsem", S.nsem)
    return nc, out_plan


def _colv(v, n):
    return np.ascontiguousarray(np.asarray(v, np.float32).reshape(n, 128).T)


def _lhsT_tiles(W, kchunks, ochunks):
    K, O = W.shape
    return np.ascontiguousarray(W.reshape(kchunks, 128, ochunks, 128).transpose(2, 1, 0, 3))


def _rhs_tiles(W, kchunks):
    K, N = W.shape
    return np.ascontiguousarray(W.reshape(kchunks, 128, N).transpose(1, 0, 2))


def _bias_tiles(rpb):
    specs = [(10, 10 + j) for j in (-2, -1, 0, 1, 2)]
    specs += [(0, b) for b in range(4)] + [(1, b) for b in range(4)]
    specs += [(30, b) for b in range(28, 32)] + [(31, b) for b in range(28, 32)]
    out = np.full((16, 128, 21, 128), NEG, np.float32)
    kc = np.arange(64)
    qc = np.arange(64)
    cs = np.clip(qc - 8, 0, 48)
    colvalid = (kc[:, None] >= cs[None, :]) & (kc[:, None] < cs[None, :] + 16)
    dcidx = np.clip(kc[:, None] - qc[None, :], -15, 15) + 15
    for ty, (m, b) in enumerate(specs):
        for pk in range(2):
            for pq in range(2):
                kr = 2 * b + pk
                r = 2 * m + pq
                kr0 = min(max(r - 4, 0), 56)
                if not (kr0 <= kr < kr0 + 8):
                    continue
                vals = rpb[:, kr - r + 7, :][:, dcidx]
                blk = np.where(colvalid[None], vals, np.float32(NEG))
                out[:, pk * 64:pk * 64 + 64, ty, pq * 64:pq * 64 + 64] = blk
    return out


def _prep_shared(inp):
    f = lambda k: np.asarray(inp[k], np.float32)
    sh = {}
    w_in = f("sg_w_in")[0]
    sh["wu"] = _lhsT_tiles(w_in[:, :2048], 8, 16)
    sh["wv"] = _rhs_tiles(w_in[:, 2048:], 8)
    sh["wso"] = _lhsT_tiles(f("sg_w_out")[0], 16, 8)
    wup = f("ffn_w_up")
    ups = []
    for l in range(2):
        a = wup[l][:, :FF].reshape(8, 128, NJ, 128).transpose(2, 1, 0, 3)
        g = wup[l][:, FF:].reshape(8, 128, NJ, 128).transpose(2, 1, 0, 3)
        ups.append(np.concatenate([a, g], axis=3))
    sh["wup"] = np.ascontiguousarray(np.stack(ups))
    sh["wdn"] = np.ascontiguousarray(np.stack([_lhsT_tiles(f("ffn_w_down")[l], NJ, 8) for l in range(2)]))
    gp = []
    for l in range(2):
        g = _lhsT_tiles(f("ple_w_gate")[l], 8, 8)
        p = _lhsT_tiles(f("ple_w_proj")[l], 2, 8)
        gp.append(np.concatenate([g, p], axis=2))
    sh["wgp"] = np.ascontiguousarray(np.stack(gp))
    wqkv = f("na_w_qkv")[0]
    sh["wqk"] = _lhsT_tiles(wqkv[:, :2048], 8, 16)
    sh["wvn"] = np.ascontiguousarray(np.stack([_rhs_tiles(wqkv[:, 2048 + vb * 512:2048 + vb * 512 + 512], 8) for vb in range(2)]))
    sh["won"] = _lhsT_tiles(f("na_w_o")[0], 8, 8)
    sh["wst"] = np.ascontiguousarray(f("sg_w_s")[0].transpose(2, 0, 1))
    cols = np.zeros((128, NCOL), np.float32)

    def put(name, v, n):
        cols[:, COLS[name]:COLS[name] + n] = _colv(v, n)

    put("bu_in", f("sg_b_in")[0][:2048], 16)
    put("lng", f("sg_ln_g")[0], 16)
    for l in range(2):
        put(f"ln1g{l}", f("ln1_g")[l], 8)
        put(f"ln1b{l}", f("ln1_b")[l], 8)
        put(f"ln2g{l}", f("ln2_g")[l], 8)
        put(f"ln2b{l}", f("ln2_b")[l], 8)
        put(f"bgate{l}", f("ple_b_gate")[l], 8)
        put(f"bua{l}", f("ffn_b_up")[l][:FF], NJ)
        put(f"bug{l}", f("ffn_b_up")[l][FF:], NJ)
        for k in range(3):
            put(f"w{k}a{l}", f("ffn_conv_w")[l][k][:FF], NJ)
            put(f"w{k}g{l}", f("ffn_conv_w")[l][k][FF:], NJ)
        put(f"cba{l}", f("ffn_conv_b")[l][:FF], NJ)
        put(f"cbg{l}", f("ffn_conv_b")[l][FF:], NJ)
    bqkv = f("na_b_qkv")[0]
    put("bq", bqkv[:1024], 8)
    put("bk", bqkv[1024:2048], 8)
    sh["cols"] = cols
    rows = np.zeros((1, NROW), np.float32)
    rows[0, R_BINV:R_BINV + 2048] = f("sg_b_in")[0][2048:]
    rows[0, R_BV:R_BV + 1024] = bqkv[2048:]
    rows[0, R_BOUT:R_BOUT + 1024] = f("sg_b_out")[0]
    rows[0, R_BDN0:R_BDN0 + 1024] = f("ffn_b_down")[0]
    rows[0, R_BDN1:R_BDN1 + 1024] = f("ffn_b_down")[1]
    rows[0, R_BO:R_BO + 1024] = f("na_b_o")[0]
    rows[0, R_BS:R_BS + 1024] = f("sg_b_s")[0].reshape(-1)
    rows[0, R_LNB:R_LNB + 2048] = f("sg_ln_b")[0]
    sh["rows"] = rows
    sh["rpbt"] = _bias_tiles(f("na_rpb")[0])
    sh["ident"] = np.eye(128, dtype=np.float32)
    return sh


def kernel(**inputs):
    stop = os.environ.get("KSTOP") or None
    ncores = int(os.environ.get("KCORES", "8"))
    sh = _prep_shared(inputs)
    x = np.asarray(inputs["x"], np.float32)
    p = np.asarray(inputs["p"], np.float32)
    in_maps = []
    for b in range(ncores):
        m = dict(sh)
        m["xT"] = np.ascontiguousarray(x[b].T).reshape(8, 128, SEQ)
        m["pT"] = np.ascontiguousarray(p[:, b].transpose(0, 2, 1)).reshape(2, 2, 128, SEQ)
        in_maps.append(m)
    nc = build_program(stop)
    res = run_bass_kernel_spmd(nc, in_maps, core_ids=list(range(ncores)))
    out = np.empty((ncores, SEQ, D), np.float32)
    for b in range(ncores):
        out[b] = res.results[b]["outT"].reshape(D, SEQ).T
    return out
```
